# Optimizing a Trainium2 kernel written in Bass

```python
import jax, jax.numpy as jnp
from jax import lax
import numpy as np

D_MODEL = 1024
BATCH = 4
SEQ = 8192
DEPTH = 4

N_GROUPS = 4
FOURIER_WIDTH = 512
FOURIER_GROUP = FOURIER_WIDTH // N_GROUPS
CONV_WIDTH = 512
CONV_K = 3
POOL_WIDTH = 512
POOL_GROUP = POOL_WIDTH // N_GROUPS
POOL_WINDOWS = (2, 4, 8, 16)
OUT_GROUP = D_MODEL // N_GROUPS
N_BRANCHES = 3
D_FF = 2816
EPS = 1e-6

OFF_F = 0
OFF_B = OFF_F + FOURIER_WIDTH
OFF_C = OFF_B + CONV_WIDTH
OFF_V = OFF_C + CONV_WIDTH
OFF_P = OFF_V + CONV_WIDTH
OFF_G = OFF_P + POOL_WIDTH
IN_WIDTH = OFF_G + N_BRANCHES * D_MODEL

kernel_name = "hybrid_fourier_conv_pool_macaron_encoder"


def rmsnorm(x, g):
    xf = x.astype(jnp.float32)
    y = xf * lax.rsqrt(jnp.mean(xf * xf, axis=-1, keepdims=True) + EPS)
    return (y * g.astype(jnp.float32)).astype(x.dtype)


def swiglu(h, w1, w3, w2):
    return (jax.nn.silu(h @ w1) * (h @ w3)) @ w2


def fourier_mix(u, w_map):
    b, s, _ = u.shape
    ug = u.reshape(b, s, N_GROUPS, FOURIER_GROUP).astype(jnp.float32)
    f = jnp.fft.fftn(ug, axes=(1, 3), norm="ortho").real.astype(u.dtype)
    y = jnp.einsum("bsgc,gcd->bsgd", f, w_map)
    return y.reshape(b, s, D_MODEL)


def short_conv(bg, cg, v, w_conv, w_out):
    z = cg * v
    zp = jnp.pad(z, ((0, 0), (1, 1), (0, 0)))
    conv = w_conv[0] * zp[:, :-2] + w_conv[1] * zp[:, 1:-1] + w_conv[2] * zp[:, 2:]
    return (bg * conv) @ w_out


def pool_mix(u, w_map, scale):
    s = u.shape[1]
    t = jnp.arange(s, dtype=jnp.float32)
    outs = []
    for i, w in enumerate(POOL_WINDOWS):
        half = w // 2
        ug = u[..., i * POOL_GROUP:(i + 1) * POOL_GROUP].astype(jnp.float32)
        cs = jnp.pad(jnp.cumsum(ug, axis=1), ((0, 0), (1, 0), (0, 0)))
        padded = jnp.pad(cs, ((0, 0), (half, half), (0, 0)), mode="edge")
        win = padded[:, w:w + s] - padded[:, :s]
        count = jnp.minimum(t + half, float(s)) - jnp.maximum(t - half, 0.0)
        pooled = win / count[None, :, None] - ug
        outs.append(pooled.astype(u.dtype) @ w_map[i])
    return jnp.concatenate(outs, axis=-1) * scale


def setup_inputs(seed: int = 0) -> dict:
    key = jax.random.key(seed)
    ks = jax.random.split(key, 20)
    f32 = jnp.float32

    def nrm(k, shape, fan_in):
        return jax.random.normal(k, shape, f32) * (fan_in ** -0.5)

    def gain(k, shape):
        return 1.0 + 0.02 * jax.random.normal(k, shape, f32)

    L, D = DEPTH, D_MODEL
    return {
        "x": jax.random.normal(ks[0], (BATCH, SEQ, D), f32),
        "g_ffn1": gain(ks[1], (L, D)),
        "w1_a": nrm(ks[2], (L, D, D_FF), D),
        "w3_a": nrm(ks[3], (L, D, D_FF), D),
        "w2_a": nrm(ks[4], (L, D_FF, D), D_FF),
        "g_mix": gain(ks[5], (L, D)),
        "w_in": nrm(ks[6], (L, D, IN_WIDTH), D),
        "w_fourier": nrm(ks[7], (L, N_GROUPS, FOURIER_GROUP, OUT_GROUP), FOURIER_GROUP),
        "w_conv": nrm(ks[8], (L, CONV_K, CONV_WIDTH), CONV_K),
        "w_conv_out": nrm(ks[9], (L, CONV_WIDTH, D), CONV_WIDTH),
        "w_pool": nrm(ks[10], (L, N_GROUPS, POOL_GROUP, OUT_GROUP), POOL_GROUP),
        "pool_scale": gain(ks[11], (L, D)),
        "w_o": nrm(ks[12], (L, D, D), D),
        "g_ffn2": gain(ks[13], (L, D)),
        "w1_b": nrm(ks[14], (L, D, D_FF), D),
        "w3_b": nrm(ks[15], (L, D, D_FF), D),
        "w2_b": nrm(ks[16], (L, D_FF, D), D_FF),
        "g_final": gain(ks[17], (D,)),
    }


def reference(x, g_ffn1, w1_a, w3_a, w2_a, g_mix, w_in, w_fourier, w_conv, w_conv_out,
              w_pool, pool_scale, w_o, g_ffn2, w1_b, w3_b, w2_b, g_final):
    b, s, d = x.shape
    for l in range(DEPTH):
        h = rmsnorm(x, g_ffn1[l])
        x = x + 0.5 * swiglu(h, w1_a[l], w3_a[l], w2_a[l])

        h = rmsnorm(x, g_mix[l])
        p = h @ w_in[l]
        gates = jax.nn.sigmoid(p[..., OFF_G:].astype(jnp.float32)).astype(x.dtype)
        gates = gates.reshape(b, s, N_BRANCHES, d)
        y_f = fourier_mix(p[..., OFF_F:OFF_B], w_fourier[l])
        y_c = short_conv(p[..., OFF_B:OFF_C], p[..., OFF_C:OFF_V], p[..., OFF_V:OFF_P],
                         w_conv[l], w_conv_out[l])
        y_p = pool_mix(p[..., OFF_P:OFF_G], w_pool[l], pool_scale[l])
        merged = gates[:, :, 0] * y_f + gates[:, :, 1] * y_c + gates[:, :, 2] * y_p
        x = x + merged @ w_o[l]

        h = rmsnorm(x, g_ffn2[l])
        x = x + 0.5 * swiglu(h, w1_b[l], w3_b[l], w2_b[l])
    return rmsnorm(x, g_final)
```

```python
import contextlib
import numpy as np
import concourse.bass as bass
import concourse.mybir as mybir
from concourse.bass_utils import run_bass_kernel_spmd

F32 = mybir.dt.float32
BF16 = mybir.dt.bfloat16
AF = mybir.ActivationFunctionType
ALU = mybir.AluOpType

D = 1024
DFF = 2816
SEQ = 8192
BATCH = 4
DEPTH = 4
INW = 5632
OFF_F, OFF_B, OFF_C, OFF_V, OFF_P, OFF_G = 0, 512, 1024, 1536, 2048, 2560
EPS = 1e-6
POOLW = (2, 4, 8, 16)

ENGS = ("pe", "act", "dve", "pool", "sp")
COMPUTE = ("pe", "act", "dve", "pool")


class Op:
    __slots__ = ("eng", "fn", "deps", "dma_sem", "dma_val", "signal", "sigval", "is_dma", "inc")

    def __init__(self, eng, fn, is_dma):
        self.eng = eng
        self.fn = fn
        self.deps = []
        self.is_dma = is_dma
        self.dma_sem = None
        self.dma_val = 0
        self.signal = False
        self.sigval = 0


class Sched:
    def __init__(self):
        self.ops = {e: [] for e in ENGS}
        self.last_write = {}
        self.readers = {}
        self.dma_sem_keys = {}
        self.last_dma = {}
        self.all_ops = []
        self.bar_pending = {}

    def add(self, eng, fn, reads=(), writes=(), dma_key=None, inc=16):
        is_dma = dma_key is not None
        op = Op(eng, fn, is_dma)
        deps = list(self.bar_pending.pop(eng, ()))
        for r in reads:
            lw = self.last_write.get(r)
            if lw is not None:
                deps.append(lw)
        for w in writes:
            lw = self.last_write.get(w)
            if lw is not None:
                deps.append(lw)
            deps.extend(self.readers.get(w, {}).values())
        seen = set()
        for d in deps:
            if d is op or id(d) in seen:
                continue
            seen.add(id(d))
            if (not d.is_dma) and d.eng == eng and (not is_dma) and eng == "pe":
                continue
            op.deps.append(d)
        for r in reads:
            self.readers.setdefault(r, {})[(eng if not is_dma else id(op))] = op
        for w in writes:
            self.last_write[w] = op
            self.readers[w] = {}
        if is_dma:
            n = self.dma_sem_keys.get(dma_key, 0) + inc
            self.dma_sem_keys[dma_key] = n
            op.dma_sem = dma_key
            op.dma_val = n
            op.inc = inc
            self.last_dma[dma_key] = op
        self.ops[eng].append(op)
        self.all_ops.append(op)
        return op

    def barrier(self):
        deps = []
        for e in COMPUTE:
            for op in reversed(self.ops[e]):
                if not op.is_dma and op.fn is not None:
                    deps.append(op)
                    break
        deps.extend(self.last_dma.values())
        self.bar_pending = {e: deps for e in ENGS}

    def emit(self, nc):
        for op in self.all_ops:
            for d in op.deps:
                if not d.is_dma:
                    d.signal = True
        for e in ENGS:
            c = 0
            for op in self.ops[e]:
                if op.signal:
                    c += 1
                    op.sigval = c
        EP = 16000
        with contextlib.ExitStack() as st:
            esem = {}
            for e in COMPUTE:
                nsig = sum(1 for op in self.ops[e] if op.signal)
                esem[e] = [st.enter_context(nc.semaphore("s_%s%d" % (e, i)))
                           for i in range(nsig // EP + 1)]
            dsem = {}
            for i, k in enumerate(self.dma_sem_keys):
                dsem[k] = st.enter_context(nc.semaphore("d%d" % i))
            block = st.enter_context(nc.Block())

            def run(engname, eng):
                waited = {}
                for op in self.ops[engname]:
                    for d in op.deps:
                        if d.is_dma:
                            sem, val = dsem[d.dma_sem], d.dma_val
                            key = ("d", d.dma_sem)
                        else:
                            ep = (d.sigval - 1) // EP
                            sem, val = esem[d.eng][ep], d.sigval - ep * EP
                            key = ("e", d.eng, ep)
                        if waited.get(key, 0) >= val:
                            continue
                        waited[key] = val
                        eng.wait_ge(sem, val)
                    if op.fn is None:
                        continue
                    ins = op.fn(eng)
                    if op.is_dma:
                        if op.inc == 1:
                            ins.then_inc(dsem[op.dma_sem])
                        else:
                            ins.then_inc(dsem[op.dma_sem], op.inc)
                    elif op.signal:
                        ins.then_inc(esem[engname][(op.sigval - 1) // EP], 1)

            @block.tensor
            def _(eng):
                run("pe", eng)

            @block.scalar
            def _(eng):
                run("act", eng)

            @block.vector
            def _(eng):
                run("dve", eng)

            @block.gpsimd
            def _(eng):
                run("pool", eng)

            @block.sync
            def _(eng):
                run("sp", eng)


def host_consts(split, rank):
    nm = 128 // split
    a = np.arange(64)[:, None].astype(np.float64)
    r = np.arange(64)[None, :].astype(np.float64)
    th = 2 * np.pi * a * r / 64.0
    d64 = np.concatenate([np.cos(th), -np.sin(th)], axis=1).astype(np.float32)
    b = np.arange(128)[:, None, None].astype(np.float64)
    rr = np.arange(64)[None, :, None].astype(np.float64)
    m = (np.arange(nm) + rank * nm)[None, None, :].astype(np.float64)
    th2 = 2 * np.pi * b * ((rr + 64.0 * m) % SEQ) / SEQ
    c2, s2 = np.cos(th2), np.sin(th2)
    mr = np.zeros((128, 64, 2, 2 * nm), np.float32)
    mr[:, :, 0, :nm] = c2
    mr[:, :, 0, nm:] = -s2
    mr[:, :, 1, :nm] = s2
    mr[:, :, 1, nm:] = c2
    c = np.arange(128)[:, None].astype(np.float64)
    cp = np.arange(128)[None, :].astype(np.float64)
    th3 = 2 * np.pi * c * cp / 128.0
    scale = 1.0 / np.sqrt(SEQ * 128.0)
    ccsc = np.stack([np.cos(th3) * scale, np.sin(th3) * scale], axis=1).astype(np.float32)
    edge = np.zeros((128, 2, 4, 8), np.float32)
    for i, w in enumerate(POOLW):
        half = w // 2
        for t in range(8):
            cl = min(t + half, SEQ) - max(t - half, 0)
            tr = SEQ - 8 + t
            cr = min(tr + half, SEQ) - max(tr - half, 0)
            edge[:, 0, i, t] = (1.0 / cl) if rank == 0 else 1.0 / w
            edge[:, 1, i, t] = (1.0 / cr) if rank == split - 1 else 1.0 / w
    import ml_dtypes
    return d64.astype(ml_dtypes.bfloat16), mr.astype(ml_dtypes.bfloat16), ccsc, edge


def pack_vecs(inp, depth):
    cols = []

    def fm(v):
        return np.ascontiguousarray(v.reshape(8, 128).T)

    for l in range(depth):
        cols.append(fm(inp["g_ffn1"][l]))
        cols.append(fm(inp["g_mix"][l]))
        cols.append(fm(inp["g_ffn2"][l]))
        cols.append(fm(inp["pool_scale"][l]))
        wc = inp["w_conv"][l]
        cols.append(np.ascontiguousarray(wc.reshape(3, 4, 128).transpose(2, 0, 1).reshape(128, 12)))
    cols.append(fm(inp["g_final"]))
    return np.ascontiguousarray(np.concatenate(cols, axis=1).astype(np.float32))


VPL = 8 * 4 + 12


def build_program(depth=DEPTH, split=1, debug_stop=None):
    T = SEQ // split
    NM = 128 // split
    TT = 2048
    NT = T // TT
    NSUB = TT // 512
    nc = bass.Bass("TRN2", target_bir_lowering=False)
    S = Sched()

    def dram_in(name, shape, dt=F32):
        return nc.dram_tensor(name, list(shape), dt, kind="ExternalInput").ap()

    xT = dram_in("xT", [D, T])
    w1 = {"a": dram_in("w1_a", [depth, D, DFF]), "b": dram_in("w1_b", [depth, D, DFF])}
    w3 = {"a": dram_in("w3_a", [depth, D, DFF]), "b": dram_in("w3_b", [depth, D, DFF])}
    w2 = {"a": dram_in("w2_a", [depth, DFF, D]), "b": dram_in("w2_b", [depth, DFF, D])}
    w_in = dram_in("w_in", [depth, D, INW])
    w_fourier = dram_in("w_fourier", [depth, 4, 128, 256])
    w_conv_out = dram_in("w_conv_out", [depth, 512, D])
    w_pool = dram_in("w_pool", [depth, 4, 128, 256])
    w_o = dram_in("w_o", [depth, D, D])
    NV = depth * VPL + 8
    vecs_d = dram_in("vecs", [128, NV])
    d64_d = dram_in("d64", [64, 128], BF16)
    mr_d = dram_in("mr", [128, 64, 2, 2 * NM], BF16)
    ccsc_d = dram_in("ccsc", [128, 2, 128])
    edge_d = dram_in("edge", [128, 2, 4, 8])
    yT = nc.dram_tensor("yT", [D, T], F32, kind="ExternalOutput").ap()

    dk = dict(kind="ExternalOutput") if debug_stop else {}
    xs = nc.dram_tensor("xs", [D, T], F32, **dk).ap()
    pfall_t = [nc.dram_tensor("pfall%d" % g, [SEQ, 128], BF16) for g in range(4)]
    pfall = [t_.ap() for t_ in pfall_t]
    if split == 1:
        pfin_t, pfin = pfall_t, pfall
    else:
        pfin_t = [nc.dram_tensor("pfin%d" % g, [T, 128], BF16) for g in range(4)]
        pfin = [t_.ap() for t_ in pfin_t]
        gin_h_t = nc.dram_tensor("gin_h", [128, 128], BF16)
        gout_h_t = nc.dram_tensor("gout_h", [256, 128], BF16)
        gin_h = gin_h_t.ap().rearrange("p (w ch t) -> p w ch t", w=2, ch=8)
        gout_h = gout_h_t.ap().rearrange("(r p) (w ch t) -> r p w ch t", r=2, w=2, ch=8)
        masks_d = dram_in("masks", [128, 2])
    GROUPS = [[2 * i, 2 * i + 1] for i in range(4)]
    zu = nc.dram_tensor("zu", [D, T + 16], BF16, **dk).ap()
    zu_b = nc.dram_tensor("zu_b", [D, T + 16], BF16).ap()
    yf = nc.dram_tensor("yf", [D, T], F32, **dk).ap()
    dbg = nc.dram_tensor("dbg", [D, T], F32, **dk).ap() if debug_stop else None

    bank_ctr = [0]

    def nextbank():
        bank_ctr[0] = (bank_ctr[0] + 1) % 8
        return bank_ctr[0]

    rot = {}

    def rotate(name, n):
        rot[name] = (rot.get(name, -1) + 1) % n
        return rot[name]

    with contextlib.ExitStack() as top:
        uniq = [0]

        def sbt(st, name, shape, dt):
            uniq[0] += 1
            return st.enter_context(nc.sbuf_tensor("sb%d_%s" % (uniq[0], name), list(shape), dt))

        ps = [top.enter_context(nc.psum_tensor("ps%d" % i, [128, 512], F32)) for i in range(8)]
        vecs = sbt(top, "vecs", [128, NV], F32)
        ones = sbt(top, "ones", [128, 128], BF16)
        edge = sbt(top, "edge", [128, 2, 4, 8], F32)
        zero16 = sbt(top, "zero16", [128, 8, 8], BF16)
        if split == 2:
            masks = sbt(top, "masks", [128, 2], F32)
            S.add("sp", lambda e: e.dma_start(out=masks[:], in_=masks_d), writes=["masks"], dma_key="masks")

        S.add("sp", lambda e: e.dma_start(out=vecs[:], in_=vecs_d), writes=["vecs"], dma_key="vecs")
        S.add("sp", lambda e: e.dma_start(out=edge[:], in_=edge_d), writes=["edge"], dma_key="edge")
        S.add("dve", lambda e: e.memset(ones[:], 1.0 / 1024.0), writes=["ones"])
        S.add("dve", lambda e: e.memset(zero16[:], 0.0), writes=["zero16"])
        zuvs = [zu.rearrange("(ch p) t -> p ch t", p=128), zu_b.rearrange("(ch p) t -> p ch t", p=128)]
        for par in range(2):
            S.add("sp", lambda e, par=par: e.dma_start(out=zuvs[par][:, :, 0:8], in_=zero16[:]), reads=["zero16"],
                  writes=[("zu_hl", par)], dma_key=("zu_hl", par))
            S.add("sp", lambda e, par=par: e.dma_start(out=zuvs[par][:, :, T + 8:T + 16], in_=zero16[:]), reads=["zero16"],
                  writes=[("zu_hr", par)], dma_key=("zu_hr", par))

        def vcol(l, which, k=0):
            base = l * VPL + {"g1": 0, "gm": 8, "g2": 16, "ps": 24, "wc": 32}[which]
            return base + k

        def rmsnorm(xt, h, sq, rt, rstd, gcol):
            for sub in range(NSUB):
                tsl = slice(sub * 512, (sub + 1) * 512)
                q = 0
                S.add("act", lambda e, q=q, tsl=tsl: e.activation(out=sq[q][:], in_=xt[:, :, tsl], func=AF.Square),
                      reads=[("xt", m, sub) for m in range(8)], writes=[("sq", q)])
                bk = nextbank()
                for kc in range(8):
                    S.add("pe", lambda e, q=q, kc=kc, bk=bk: e.matmul(ps[bk][:], ones[:], sq[q][:, kc, :],
                                                                       start=(kc == 0), stop=(kc == 7)),
                          reads=[("sq", q), "ones"], writes=[("ps", bk)])
                S.add("act", lambda e, bk=bk: e.activation(out=rt[:], in_=ps[bk][:], func=AF.Ln, bias=EPS, scale=1.0),
                      reads=[("ps", bk)], writes=["rt"])
                S.add("act", lambda e: e.activation(out=rstd[:], in_=rt[:], func=AF.Exp, scale=-0.5),
                      reads=["rt"], writes=["rstd"])
                for kc in range(8):
                    S.add("dve", lambda e, kc=kc, tsl=tsl: e.scalar_tensor_tensor(
                        out=h[:, kc, tsl], in0=xt[:, kc, tsl], scalar=vecs[:, gcol + kc:gcol + kc + 1], in1=rstd[:],
                        op0=ALU.mult, op1=ALU.mult),
                        reads=[("xt", kc, sub), "rstd", "vecs"], writes=[("h", sub)])

        def ffn(l, ab, xt, h):
            groups = [(0, 4), (4, 4), (8, 4), (12, 4), (16, 4), (20, 2)]
            with contextlib.ExitStack() as st:
                w1b = [sbt(st, "w1b%d" % i, [128, 8, 512], BF16) for i in range(2)]
                w3b = [sbt(st, "w3b%d" % i, [128, 8, 512], BF16) for i in range(2)]
                w2b = [sbt(st, "w2b%d" % i, [128, 4, 1024], BF16) for i in range(2)]
                ub = [sbt(st, "ub%d" % i, [128, 4, TT], BF16) for i in range(2)]
                sg = [sbt(st, "sg%d" % i, [128, 512], F32) for i in range(2)]
                for (j0, nj) in groups:
                    b = rotate("ffnw", 2)
                    c0, c1 = j0 * 128, (j0 + nj) * 128
                    S.add("pool", lambda e, b=b, c0=c0, c1=c1, nj=nj: e.dma_start(
                        out=w1b[b][:, :, 0:nj * 128], in_=w1[ab][l, :, c0:c1].rearrange("(kc p) n -> p kc n", p=128)),
                        writes=[("w1b", b)], dma_key=("w1b", b))
                    S.add("pool", lambda e, b=b, c0=c0, c1=c1, nj=nj: e.dma_start(
                        out=w3b[b][:, :, 0:nj * 128], in_=w3[ab][l, :, c0:c1].rearrange("(kc p) n -> p kc n", p=128)),
                        writes=[("w3b", b)], dma_key=("w3b", b))
                    S.add("pool", lambda e, b=b, c0=c0, c1=c1, nj=nj: e.dma_start(
                        out=w2b[b][:, 0:nj, :], in_=w2[ab][l, c0:c1, :].rearrange("(j p) n -> p j n", p=128)),
                        writes=[("w2b", b)], dma_key=("w2b", b))
                    up_order = ([(jj, sub) for sub in range(NSUB) for jj in range(nj)] if j0 == 0
                                else [(jj, sub) for jj in range(nj) for sub in range(NSUB)])
                    for (jj, sub) in up_order:
                        if True:
                            tsl = slice(sub * 512, (sub + 1) * 512)
                            ba, bb = nextbank(), nextbank()
                            for kc in range(8):
                                S.add("pe", lambda e, b=b, kc=kc, jj=jj, tsl=tsl, ba=ba: e.matmul(
                                    ps[ba][:], w1b[b][:, kc, jj * 128:(jj + 1) * 128], h[:, kc, tsl],
                                    start=(kc == 0), stop=(kc == 7)),
                                    reads=[("w1b", b), ("h", sub)], writes=[("ps", ba)])
                            for kc in range(8):
                                S.add("pe", lambda e, b=b, kc=kc, jj=jj, tsl=tsl, bb=bb: e.matmul(
                                    ps[bb][:], w3b[b][:, kc, jj * 128:(jj + 1) * 128], h[:, kc, tsl],
                                    start=(kc == 0), stop=(kc == 7)),
                                    reads=[("w3b", b), ("h", sub)], writes=[("ps", bb)])
                            k = rotate("sg", 2)
                            S.add("act", lambda e, k=k, ba=ba: e.activation(out=sg[k][:], in_=ps[ba][:], func=AF.Silu),
                                  reads=[("ps", ba)], writes=[("sg", k)])
                            S.add("dve", lambda e, k=k, bb=bb, b=b, jj=jj, tsl=tsl: e.tensor_tensor(
                                out=ub[b][:, jj, tsl], in0=sg[k][:], in1=ps[bb][:], op=ALU.mult),
                                reads=[("sg", k), ("ps", bb)], writes=[("ub", b, jj, sub)])
                    for m in range(8):
                        for sub in range(NSUB):
                            tsl = slice(sub * 512, (sub + 1) * 512)
                            bo = nextbank()
                            for jj in range(nj):
                                S.add("pe", lambda e, b=b, jj=jj, m=m, tsl=tsl, bo=bo, nj=nj: e.matmul(
                                    ps[bo][:], w2b[b][:, jj, m * 128:(m + 1) * 128], ub[b][:, jj, tsl],
                                    start=(jj == 0), stop=(jj == nj - 1)),
                                    reads=[("w2b", b), ("ub", b, jj, sub)], writes=[("ps", bo)])
                            S.add("dve", lambda e, m=m, tsl=tsl, bo=bo: e.scalar_tensor_tensor(
                                out=xt[:, m, tsl], in0=ps[bo][:], scalar=0.5, in1=xt[:, m, tsl],
                                op0=ALU.mult, op1=ALU.add),
                                reads=[("ps", bo), ("xt", m, sub)], writes=[("xt", m, sub)])
            S.barrier()

        def pre_mixer(l, tile, h):
            t0 = tile * TT
            with contextlib.ExitStack() as st:
                winf = sbt(st, "winf", [128, 8, 512], BF16)
                pfst = [sbt(st, "pfst%d" % i, [128, 4, 512], BF16) for i in range(2)]
                wch = {k: [sbt(st, "wch%s%d" % (k, i), [128, 8, 128], BF16) for i in range(2)] for k in "CVP"}
                zst = sbt(st, "zst", [128, 8, TT], BF16)
                ctmp = [sbt(st, "ctmp%d" % i, [128, 512], F32) for i in range(2)]
                S.add("pool", lambda e: e.dma_start(
                    out=winf[:], in_=w_in[l, :, OFF_F:OFF_F + 512].rearrange("(kc p) n -> p kc n", p=128)),
                    writes=["winf"], dma_key="winf")
                for s4 in range(NSUB):
                    k = rotate("pfst", 2)
                    for tb in range(4):
                        tok = s4 * 512 + tb * 128
                        bk = nextbank()
                        for kc in range(8):
                            S.add("pe", lambda e, kc=kc, tok=tok, bk=bk: e.matmul(
                                ps[bk][:], h[:, kc, tok:tok + 128], winf[:, kc, :], start=(kc == 0), stop=(kc == 7)),
                                reads=[("h", s4), "winf"], writes=[("ps", bk)])
                        if tb % 2 == 0:
                            S.add("act", lambda e, k=k, tb=tb, bk=bk: e.copy(out=pfst[k][:, tb, :], in_=ps[bk][:]),
                                  reads=[("ps", bk)], writes=[("pfst", k, tb)])
                        else:
                            S.add("dve", lambda e, k=k, tb=tb, bk=bk: e.tensor_copy(out=pfst[k][:, tb, :], in_=ps[bk][:]),
                                  reads=[("ps", bk)], writes=[("pfst", k, tb)])
                    for tb in range(4):
                        tok = t0 + s4 * 512 + tb * 128
                        for g in range(4):
                            S.add("sp", lambda e, k=k, tb=tb, tok=tok, g=g: e.dma_start(
                                out=pfin[g][tok:tok + 128, :], in_=pfst[k][:, tb, g * 128:(g + 1) * 128]),
                                reads=[("pfst", k, tb)], writes=[("pf", g, tile, s4, tb)], dma_key=("pfst", k, tb))
                for cc in range(4):
                    b = rotate("wch", 2)
                    for kind, off in (("C", OFF_C), ("V", OFF_V), ("P", OFF_P)):
                        S.add("pool", lambda e, kind=kind, off=off, b=b, cc=cc: e.dma_start(
                            out=wch[kind][b][:],
                            in_=w_in[l, :, off + cc * 128:off + (cc + 1) * 128].rearrange("(kc p) n -> p kc n", p=128)),
                            writes=[("wch", kind, b)], dma_key=("wch", kind, b))
                    for sub in range(NSUB):
                        tsl = slice(sub * 512, (sub + 1) * 512)
                        bks = {}
                        for kind in "CVP":
                            bk = nextbank()
                            bks[kind] = bk
                            for kc in range(8):
                                S.add("pe", lambda e, kind=kind, b=b, kc=kc, tsl=tsl, bk=bk: e.matmul(
                                    ps[bk][:], wch[kind][b][:, kc, :], h[:, kc, tsl], start=(kc == 0), stop=(kc == 7)),
                                    reads=[("wch", kind, b), ("h", sub)], writes=[("ps", bk)])
                        k = rotate("ctmp", 2)
                        S.add("act", lambda e, k=k, bk=bks["C"]: e.copy(out=ctmp[k][:], in_=ps[bk][:]),
                              reads=[("ps", bks["C"])], writes=[("ctmp", k)])
                        S.add("dve", lambda e, k=k, bk=bks["V"], cc=cc, tsl=tsl: e.tensor_tensor(
                            out=zst[:, cc, tsl], in0=ctmp[k][:], in1=ps[bk][:], op=ALU.mult),
                            reads=[("ctmp", k), ("ps", bks["V"])], writes=[("zst", cc)])
                        S.add("act", lambda e, bk=bks["P"], cc=cc, tsl=tsl: e.copy(out=zst[:, 4 + cc, tsl], in_=ps[bk][:]),
                              reads=[("ps", bks["P"])], writes=[("zst", 4 + cc)])
                if split == 2 and tile == NT - 1:
                    exchange_pf()
                zuv = zuvs[l % 2]
                S.add("sp", lambda e: e.dma_start(out=zuv[:, :, 8 + t0:8 + t0 + TT], in_=zst[:]),
                      reads=[("zst", i) for i in range(8)], writes=[("zu", l % 2, tile)], dma_key="zst")
                if split == 2 and tile == 0:
                    S.add("sp", lambda e: e.dma_start(out=gin_h[:, 0], in_=zst[:, :, 0:8]),
                          reads=[("zst", i) for i in range(8)], writes=["gin_h0"], dma_key="gh0")
                if split == 2 and tile == NT - 1:
                    S.add("sp", lambda e: e.dma_start(out=gin_h[:, 1], in_=zst[:, :, TT - 8:TT]),
                          reads=[("zst", i) for i in range(8)], writes=["gin_h1"], dma_key="gh1")
                    exchange_halo()
            S.barrier()

        def exchange_pf():
            for g in range(4):
                S.add("pool", lambda e, g=g: e.collective_compute(
                    "AllGather", ALU.bypass, replica_groups=GROUPS,
                    ins=[pfin_t[g].ap().opt()], outs=[pfall_t[g].ap().opt()]),
                    reads=[("pf", g, t, a_, b_) for t in range(NT) for a_ in range(4) for b_ in range(4)],
                    writes=[("pfall", g)], dma_key=("cc", g), inc=1)
        def exchange_halo():
            S.add("pool", lambda e: e.collective_compute(
                "AllGather", ALU.bypass, replica_groups=GROUPS,
                ins=[gin_h_t.ap().opt()], outs=[gout_h_t.ap().opt()]),
                reads=["gin_h0", "gin_h1"], writes=["gout_h"], dma_key=("cc", "h"), inc=1)

        def dft(l):
            with contextlib.ExitStack() as st:
                U1 = [sbt(st, "U1_%d" % i, [64, 128, 128], BF16) for i in range(split)]
                G = sbt(st, "G", [128, 128, 128], BF16)
                MRb = sbt(st, "MRb", [128, 64, 2, 2 * NM], BF16)
                Zb = sbt(st, "Zb", [128, 2, T], BF16)
                yst = [sbt(st, "yst%d" % i, [128, 1024], F32) for i in range(2)]
                d64 = sbt(st, "d64", [64, 128], BF16)
                ccsc = sbt(st, "ccsc", [128, 2, 128], F32)
                wf32 = sbt(st, "wf32", [128, 4, 256], F32)
                wcs = sbt(st, "wcs", [128, 4, 2, 256], BF16)
                S.add("sp", lambda e: e.dma_start(out=d64[:], in_=d64_d), writes=["d64"], dma_key="d64")
                S.add("pool", lambda e: e.dma_start(out=MRb[:], in_=mr_d), writes=["MRb"], dma_key="MRb")
                S.add("sp", lambda e: e.dma_start(out=ccsc[:], in_=ccsc_d), writes=["ccsc"], dma_key="ccsc")
                S.add("sp", lambda e: e.dma_start(out=wf32[:], in_=w_fourier[l].rearrange("g c d -> c g d")),
                      writes=["wf32"], dma_key="wf32")
                for g in range(4):
                    for part in range(2):
                        bk = nextbank()
                        S.add("pe", lambda e, g=g, part=part, bk=bk: e.matmul(
                            ps[bk][:, 0:256], ccsc[:, part, :], wf32[:, g, :], start=True, stop=True),
                            reads=["ccsc", "wf32"], writes=[("ps", bk)])
                        S.add("dve", lambda e, g=g, part=part, bk=bk: e.tensor_copy(out=wcs[:, g, part, :], in_=ps[bk][:, 0:256]),
                              reads=[("ps", bk)], writes=[("wcs", g)])
                def load_u1(g):
                    ub_ = g % split
                    S.add("sp", lambda e, g=g, ub_=ub_: e.dma_start(
                        out=U1[ub_][:], in_=pfall[g].rearrange("(a b) c -> a b c", b=128)),
                        reads=([("pf", g, t, a_, b_) for t in range(NT) for a_ in range(4) for b_ in range(4)]
                               if split == 1 else [("pfall", g)]),
                        writes=[("U1", ub_)], dma_key=("U1", ub_))

                if split == 2:
                    load_u1(0)
                for g in range(4):
                    ub_ = g % split
                    if split == 2:
                        if g + 1 < 4:
                            load_u1(g + 1)
                    else:
                        load_u1(g)
                    for c4 in range(32):
                        bk = nextbank()
                        for ci in range(4):
                            c = c4 * 4 + ci
                            S.add("pe", lambda e, ub_=ub_, c=c, ci=ci, bk=bk: e.matmul(
                                ps[bk][:, ci * 128:(ci + 1) * 128], U1[ub_][:, :, c], d64[:], start=True, stop=True),
                                reads=[("U1", ub_), "d64"], writes=[("ps", bk)])
                        outap = G[:, c4 * 4:(c4 + 1) * 4, :]
                        if c4 % 2 == 0:
                            S.add("act", lambda e, outap=outap, bk=bk: e.copy(
                                out=outap, in_=ps[bk][:].rearrange("p (c n) -> p c n", n=128)),
                                reads=[("ps", bk)], writes=[("G", c4)])
                        else:
                            S.add("dve", lambda e, outap=outap, bk=bk: e.tensor_copy(
                                out=outap, in_=ps[bk][:].rearrange("p (c n) -> p c n", n=128)),
                                reads=[("ps", bk)], writes=[("G", c4)])
                    Zv = Zb[:].rearrange("p t (m r) -> p t m r", r=64)
                    allG = [("G", i) for i in range(32)]
                    for r in range(64):
                        bk = nextbank()
                        for part in range(2):
                            S.add("pe", lambda e, r=r, part=part, bk=bk: e.matmul(
                                ps[bk][:, 0:2 * NM], G[:, :, part * 64 + r], MRb[:, r, part, :],
                                start=(part == 0), stop=(part == 1)),
                                reads=allG + ["MRb"], writes=[("ps", bk)])
                        src = lambda bk=bk: ps[bk][:, 0:2 * NM].rearrange("p (t m) -> p t m", t=2)
                        if r % 2 == 0:
                            S.add("act", lambda e, r=r, src=src: e.copy(out=Zv[:, :, :, r], in_=src()),
                                  reads=[("ps", bk)], writes=[("Z", r)])
                        else:
                            S.add("dve", lambda e, r=r, src=src: e.tensor_copy(out=Zv[:, :, :, r], in_=src()),
                                  reads=[("ps", bk)], writes=[("Z", r)])
                    allZ = [("Z", i) for i in range(64)]
                    for dc in range(2):
                        for k4 in range(T // 1024):
                            ys = rotate("yst", 2)
                            for kk in range(2):
                                kt = k4 * 2 + kk
                                ksl = slice(kt * 512, (kt + 1) * 512)
                                bk = nextbank()
                                for part in range(2):
                                    S.add("pe", lambda e, g=g, dc=dc, part=part, ksl=ksl, bk=bk: e.matmul(
                                        ps[bk][:], wcs[:, g, part, dc * 128:(dc + 1) * 128], Zb[:, part, ksl],
                                        start=(part == 0), stop=(part == 1)),
                                        reads=allZ + [("wcs", g)], writes=[("ps", bk)])
                                if kk % 2 == 0:
                                    S.add("act", lambda e, ys=ys, kk=kk, bk=bk: e.copy(
                                        out=yst[ys][:, kk * 512:(kk + 1) * 512], in_=ps[bk][:]),
                                        reads=[("ps", bk)], writes=[("yst", ys, kk)])
                                else:
                                    S.add("dve", lambda e, ys=ys, kk=kk, bk=bk: e.tensor_copy(
                                        out=yst[ys][:, kk * 512:(kk + 1) * 512], in_=ps[bk][:]),
                                        reads=[("ps", bk)], writes=[("yst", ys, kk)])
                            row = (2 * g + dc) * 128
                            S.add("sp", lambda e, ys=ys, row=row, k4=k4: e.dma_start(
                                out=yf[row:row + 128, k4 * 1024:(k4 + 1) * 1024], in_=yst[ys][:]),
                                reads=[("yst", ys, i) for i in range(2)], writes=[("yf", 2 * g + dc, k4)],
                                dma_key=("yst", ys))
            S.barrier()

        def post_mixer(l, tile, xt, h):
            for hf in range(TT // 1024):
                post_half(l, tile, xt, h, hf)

        def post_half(l, tile, xt, h, hf):
            HT = 1024
            if True:
                t0 = tile * TT + hf * HT
                with contextlib.ExitStack() as st:
                    zub = sbt(st, "zub", [128, 8, HT + 16], BF16)
                    bcb = sbt(st, "bcb", [128, 4, HT], BF16)
                    plb = sbt(st, "plb", [128, 4, HT], BF16)
                    mgb = sbt(st, "mgb", [128, 8, HT], BF16)
                    yfb = [sbt(st, "yfb%d" % i, [128, 512], F32) for i in range(2)]
                    wB = [sbt(st, "wB%d" % i, [128, 8, 128], BF16) for i in range(2)]
                    wg = [[sbt(st, "wg%d_%d" % (br, i), [128, 8, 128], BF16) for i in range(2)] for br in range(3)]
                    wco = [sbt(st, "wco%d" % i, [128, 4, 128], BF16) for i in range(2)]
                    wpl = sbt(st, "wpl", [128, 4, 256], BF16)
                    woc = [sbt(st, "woc%d" % i, [128, 8, 128], BF16) for i in range(2)]
                    cv = [sbt(st, "cv%d" % i, [128, 512], F32) for i in range(2)]
                    pa = sbt(st, "pa", [128, 528], F32)
                    pb = sbt(st, "pb", [128, 528], F32)
                    e8 = sbt(st, "e8", [128, 8], F32)
                    gsb = [[sbt(st, "gsb%d_%d" % (br, i), [128, 512], F32) for i in range(1)] for br in range(3)]
                    t1 = [sbt(st, "t1_%d" % i, [128, 512], F32) for i in range(1)]
                    t2 = [sbt(st, "t2_%d" % i, [128, 512], F32) for i in range(1)]

                    zuv = zuvs[l % 2]
                    S.add("sp", lambda e: e.dma_start(out=zub[:], in_=zuv[:, :, t0:t0 + HT + 16]),
                          reads=[("zu", l % 2, t) for t in range(NT)] + [("zu_hl", l % 2), ("zu_hr", l % 2)],
                          writes=["zub"], dma_key="zub")
                    if split == 2 and (t0 == 0 or t0 + HT == T):
                        hst = sbt(st, "hst", [128, 8, 8], BF16)
                        if t0 == 0:
                            S.add("sp", lambda e: e.dma_start(out=hst[:], in_=gout_h[0, :, 1]),
                                  reads=["gout_h"], writes=["hst"], dma_key="hst")
                            S.add("dve", lambda e: e.tensor_scalar(
                                out=zub[:, :, 0:8], in0=hst[:], scalar1=masks[:, 0:1], scalar2=None, op0=ALU.mult),
                                reads=["hst", "masks", "zub"], writes=["zub"])
                        else:
                            S.add("sp", lambda e: e.dma_start(out=hst[:], in_=gout_h[1, :, 0]),
                                  reads=["gout_h"], writes=["hst"], dma_key="hst")
                            S.add("dve", lambda e: e.tensor_scalar(
                                out=zub[:, :, HT + 8:HT + 16], in0=hst[:], scalar1=masks[:, 1:2], scalar2=None, op0=ALU.mult),
                                reads=["hst", "masks", "zub"], writes=["zub"])
                    S.add("pool", lambda e: e.dma_start(out=wpl[:], in_=w_pool[l].rearrange("g c d -> c g d")),
                          writes=["wpl"], dma_key="wpl")
                    for cc in range(4):
                        b = rotate("wB", 2)
                        S.add("pool", lambda e, b=b, cc=cc: e.dma_start(
                            out=wB[b][:],
                            in_=w_in[l, :, OFF_B + cc * 128:OFF_B + (cc + 1) * 128].rearrange("(kc p) n -> p kc n", p=128)),
                            writes=[("wB", b)], dma_key=("wB", b))
                        for s2 in range(2):
                            sub = hf * 2 + s2
                            o = 8 + s2 * 512
                            k = rotate("cv", 2)
                            wc0 = vcol(l, "wc", 0 * 4 + cc)
                            wc1 = vcol(l, "wc", 1 * 4 + cc)
                            wc2 = vcol(l, "wc", 2 * 4 + cc)
                            S.add("dve", lambda e, k=k, cc=cc, o=o, wc1=wc1: e.tensor_scalar(
                                out=cv[k][:], in0=zub[:, cc, o:o + 512], scalar1=vecs[:, wc1:wc1 + 1], scalar2=None,
                                op0=ALU.mult), reads=["zub", "vecs"], writes=[("cv", k)])
                            S.add("dve", lambda e, k=k, cc=cc, o=o, wc0=wc0: e.scalar_tensor_tensor(
                                out=cv[k][:], in0=zub[:, cc, o - 1:o + 511], scalar=vecs[:, wc0:wc0 + 1], in1=cv[k][:],
                                op0=ALU.mult, op1=ALU.add), reads=["zub", "vecs", ("cv", k)], writes=[("cv", k)])
                            S.add("dve", lambda e, k=k, cc=cc, o=o, wc2=wc2: e.scalar_tensor_tensor(
                                out=cv[k][:], in0=zub[:, cc, o + 1:o + 513], scalar=vecs[:, wc2:wc2 + 1], in1=cv[k][:],
                                op0=ALU.mult, op1=ALU.add), reads=["zub", "vecs", ("cv", k)], writes=[("cv", k)])
                            bk = nextbank()
                            tsl = slice(sub * 512, (sub + 1) * 512)
                            for kc in range(8):
                                S.add("pe", lambda e, b=b, kc=kc, tsl=tsl, bk=bk: e.matmul(
                                    ps[bk][:], wB[b][:, kc, :], h[:, kc, tsl], start=(kc == 0), stop=(kc == 7)),
                                    reads=[("wB", b), ("h", sub)], writes=[("ps", bk)])
                            S.add("dve", lambda e, k=k, cc=cc, s2=s2, bk=bk: e.tensor_tensor(
                                out=bcb[:, cc, s2 * 512:(s2 + 1) * 512], in0=cv[k][:], in1=ps[bk][:], op=ALU.mult),
                                reads=[("cv", k), ("ps", bk)], writes=[("bcb", s2)])
                    for i, w in enumerate(POOLW):
                        for s2 in range(2):
                            o = s2 * 512
                            ue = lambda lo, hi, i=i, o=o: zub[:, 4 + i, o + lo:o + hi]
                            R = ["zub"]
                            if i == 0:
                                S.add("dve", lambda e, ue=ue: e.tensor_tensor(out=pa[:, 8:520], in0=ue(7, 519), in1=ue(8, 520), op=ALU.add),
                                      reads=R, writes=["pa"])
                                win = pa
                            else:
                                S.add("dve", lambda e, ue=ue: e.tensor_tensor(out=pa[:, 1:528], in0=ue(0, 527), in1=ue(1, 528), op=ALU.add),
                                      reads=R, writes=["pa"])
                                S.add("dve", lambda e: e.tensor_tensor(out=pb[:, 2:527], in0=pa[:, 1:526], in1=pa[:, 3:528], op=ALU.add),
                                      reads=["pa"], writes=["pb"])
                                win = pb
                                if i >= 2:
                                    S.add("dve", lambda e: e.tensor_tensor(out=pa[:, 4:525], in0=pb[:, 2:523], in1=pb[:, 6:527], op=ALU.add),
                                          reads=["pb"], writes=["pa"])
                                    win = pa
                                if i >= 3:
                                    S.add("dve", lambda e: e.tensor_tensor(out=pb[:, 8:521], in0=pa[:, 4:517], in1=pa[:, 12:525], op=ALU.add),
                                          reads=["pa"], writes=["pb"])
                                    win = pb
                            wname = "pa" if win is pa else "pb"
                            S.add("dve", lambda e, win=win, ue=ue, i=i, s2=s2, w=w: e.scalar_tensor_tensor(
                                out=plb[:, i, s2 * 512:(s2 + 1) * 512], in0=win[:, 8:520], scalar=1.0 / w, in1=ue(8, 520),
                                op0=ALU.mult, op1=ALU.subtract), reads=[wname, "zub"], writes=[("plb", s2)])
                            if t0 == 0 and s2 == 0:
                                S.add("dve", lambda e, win=win, i=i: e.tensor_tensor(
                                    out=e8[:], in0=win[:, 8:16], in1=edge[:, 0, i, :], op=ALU.mult),
                                    reads=[wname, "edge"], writes=["e8"])
                                S.add("dve", lambda e, ue=ue, i=i: e.tensor_tensor(
                                    out=plb[:, i, 0:8], in0=e8[:], in1=ue(8, 16), op=ALU.subtract),
                                    reads=["e8", "zub"], writes=[("plb", s2)])
                            if t0 + HT == T and s2 == 1:
                                S.add("dve", lambda e, win=win, i=i: e.tensor_tensor(
                                    out=e8[:], in0=win[:, 512:520], in1=edge[:, 1, i, :], op=ALU.mult),
                                    reads=[wname, "edge"], writes=["e8"])
                                S.add("dve", lambda e, ue=ue, i=i: e.tensor_tensor(
                                    out=plb[:, i, HT - 8:HT], in0=e8[:], in1=ue(512, 520), op=ALU.subtract),
                                    reads=["e8", "zub"], writes=[("plb", s2)])
                    for m in range(8):
                        b = rotate("wg", 2)
                        for br in range(3):
                            col = OFF_G + br * 1024 + m * 128
                            S.add("pool", lambda e, br=br, b=b, col=col: e.dma_start(
                                out=wg[br][b][:], in_=w_in[l, :, col:col + 128].rearrange("(kc p) n -> p kc n", p=128)),
                                writes=[("wg", br, b)], dma_key=("wg", br, b))
                        S.add("pool", lambda e, b=b, m=m: e.dma_start(
                            out=wco[b][:], in_=w_conv_out[l, :, m * 128:(m + 1) * 128].rearrange("(j p) n -> p j n", p=128)),
                            writes=[("wco", b)], dma_key=("wco", b))
                        for s2 in range(2):
                            yb = rotate("yfb", 2)
                            S.add("sp", lambda e, yb=yb, m=m, s2=s2: e.dma_start(
                                out=yfb[yb][:], in_=yf[m * 128:(m + 1) * 128, t0 + s2 * 512:t0 + (s2 + 1) * 512]),
                                reads=[("yf", m, t0 // 1024)], writes=[("yfb", yb)], dma_key=("yfb", yb))
                            sub = hf * 2 + s2
                            tsl = slice(sub * 512, (sub + 1) * 512)
                            hsl = slice(s2 * 512, (s2 + 1) * 512)
                            gb = []
                            for br in range(3):
                                bk = nextbank()
                                gb.append(bk)
                                for kc in range(8):
                                    S.add("pe", lambda e, br=br, b=b, kc=kc, tsl=tsl, bk=bk: e.matmul(
                                        ps[bk][:], wg[br][b][:, kc, :], h[:, kc, tsl], start=(kc == 0), stop=(kc == 7)),
                                        reads=[("wg", br, b), ("h", sub)], writes=[("ps", bk)])
                            byc = nextbank()
                            for j in range(4):
                                S.add("pe", lambda e, j=j, b=b, hsl=hsl, byc=byc: e.matmul(
                                    ps[byc][:], wco[b][:, j, :], bcb[:, j, hsl], start=(j == 0), stop=(j == 3)),
                                    reads=[("wco", b), ("bcb", s2)], writes=[("ps", byc)])
                            byp = nextbank()
                            S.add("pe", lambda e, m=m, hsl=hsl, byp=byp: e.matmul(
                                ps[byp][:], wpl[:, m // 2, (m % 2) * 128:(m % 2 + 1) * 128], plb[:, m // 2, hsl],
                                start=True, stop=True),
                                reads=["wpl", ("plb", s2)], writes=[("ps", byp)])
                            k = 0
                            kk2 = 0
                            for br in range(3):
                                S.add("act", lambda e, br=br, k=k, bk=gb[br]: e.activation(
                                    out=gsb[br][k][:], in_=ps[bk][:], func=AF.Sigmoid),
                                    reads=[("ps", gb[br])], writes=[("gsb", br, k)])
                            psc = vcol(l, "ps", m)
                            q2 = kk2
                            S.add("dve", lambda e, q2=q2, byc=byc: e.tensor_tensor(
                                out=t1[q2][:], in0=gsb[1][0][:], in1=ps[byc][:], op=ALU.mult),
                                reads=[("gsb", 1, 0), ("ps", byc)], writes=[("t1", q2)])
                            S.add("dve", lambda e, q2=q2, byp=byp, psc=psc: e.scalar_tensor_tensor(
                                out=t2[q2][:], in0=ps[byp][:], scalar=vecs[:, psc:psc + 1], in1=gsb[2][0][:],
                                op0=ALU.mult, op1=ALU.mult),
                                reads=[("gsb", 2, 0), ("ps", byp), "vecs"], writes=[("t2", q2)])
                            S.add("dve", lambda e, q2=q2: e.tensor_tensor(out=t1[q2][:], in0=t1[q2][:], in1=t2[q2][:], op=ALU.add),
                                  reads=[("t1", q2), ("t2", q2)], writes=[("t1", q2)])
                            S.add("dve", lambda e, q2=q2, yb=yb: e.tensor_tensor(
                                out=t2[q2][:], in0=gsb[0][0][:], in1=yfb[yb][:], op=ALU.mult),
                                reads=[("gsb", 0, 0), ("yfb", yb)], writes=[("t2", q2)])
                            S.add("dve", lambda e, q2=q2, m=m, hsl=hsl: e.tensor_tensor(
                                out=mgb[:, m, hsl], in0=t1[q2][:], in1=t2[q2][:], op=ALU.add),
                                reads=[("t1", q2), ("t2", q2)], writes=[("mgb", m, s2)])
                    for m2 in range(8):
                        b = rotate("woc", 2)
                        S.add("pool", lambda e, b=b, m2=m2: e.dma_start(
                            out=woc[b][:], in_=w_o[l, :, m2 * 128:(m2 + 1) * 128].rearrange("(kc p) n -> p kc n", p=128)),
                            writes=[("woc", b)], dma_key=("woc", b))
                        for s2 in range(2):
                            sub = hf * 2 + s2
                            tsl = slice(sub * 512, (sub + 1) * 512)
                            hsl = slice(s2 * 512, (s2 + 1) * 512)
                            bk = nextbank()
                            for kc in range(8):
                                S.add("pe", lambda e, b=b, kc=kc, hsl=hsl, bk=bk: e.matmul(
                                    ps[bk][:], woc[b][:, kc, :], mgb[:, kc, hsl], start=(kc == 0), stop=(kc == 7)),
                                    reads=[("woc", b), ("mgb", kc, s2)], writes=[("ps", bk)])
                            S.add("dve", lambda e, m2=m2, tsl=tsl, bk=bk: e.tensor_tensor(
                                out=xt[:, m2, tsl], in0=ps[bk][:], in1=xt[:, m2, tsl], op=ALU.add),
                                reads=[("ps", bk), ("xt", m2, sub)], writes=[("xt", m2, sub)])
                S.barrier()

        def final_norm(tile, xt, h_unused, sq, rt, rstd):
            gcol = depth * VPL
            with contextlib.ExitStack() as st:
                ost = [sbt(st, "ost%d" % i, [128, 8, 512], F32) for i in range(2)]
                for sub in range(NSUB):
                    tsl = slice(sub * 512, (sub + 1) * 512)
                    q = 0
                    S.add("act", lambda e, q=q, tsl=tsl: e.activation(out=sq[q][:], in_=xt[:, :, tsl], func=AF.Square),
                          reads=[("xt", m, sub) for m in range(8)], writes=[("sq", q)])
                    bk = nextbank()
                    for kc in range(8):
                        S.add("pe", lambda e, q=q, kc=kc, bk=bk: e.matmul(ps[bk][:], ones[:], sq[q][:, kc, :],
                                                                           start=(kc == 0), stop=(kc == 7)),
                              reads=[("sq", q), "ones"], writes=[("ps", bk)])
                    S.add("act", lambda e, bk=bk: e.activation(out=rt[:], in_=ps[bk][:], func=AF.Ln, bias=EPS, scale=1.0),
                          reads=[("ps", bk)], writes=["rt"])
                    S.add("act", lambda e: e.activation(out=rstd[:], in_=rt[:], func=AF.Exp, scale=-0.5),
                          reads=["rt"], writes=["rstd"])
                    o = rotate("ost", 2)
                    for kc in range(8):
                        S.add("dve", lambda e, kc=kc, tsl=tsl, o=o: e.scalar_tensor_tensor(
                            out=ost[o][:, kc, :], in0=xt[:, kc, tsl], scalar=vecs[:, gcol + kc:gcol + kc + 1], in1=rstd[:],
                            op0=ALU.mult, op1=ALU.mult),
                            reads=[("xt", kc, sub), "rstd", "vecs"], writes=[("ost", o)])
                    c0 = tile * TT + sub * 512
                    S.add("sp", lambda e, o=o, c0=c0: e.dma_start(
                        out=yT[:, c0:c0 + 512].rearrange("(kc p) t -> p kc t", p=128), in_=ost[o][:]),
                        reads=[("ost", o)], writes=["yT"], dma_key=("ost", o))
            S.barrier()

        def run_pass(p):
            with contextlib.ExitStack() as pst:
                xt = sbt(pst, "xt", [128, 8, TT], F32)
                h = sbt(pst, "h", [128, 8, TT], BF16)
                sq = [sbt(pst, "sq%d" % i, [128, 8, 512], BF16) for i in range(1)]
                rt = sbt(pst, "rt", [128, 512], F32)
                rstd = sbt(pst, "rstd", [128, 512], F32)
                for tile in range(NT):
                    src = xT if p == 0 else xs
                    for sub in range(NSUB):
                        c0 = tile * TT + sub * 512
                        S.add("sp", lambda e, src=src, c0=c0, sub=sub: e.dma_start(
                            out=xt[:, :, sub * 512:(sub + 1) * 512],
                            in_=src[:, c0:c0 + 512].rearrange("(kc p) t -> p kc t", p=128)),
                            reads=[("xs", tile)],
                            writes=[("xt", m, sub) for m in range(8)], dma_key=("xt", sub))
                    if p > 0:
                        rmsnorm(xt, h, sq, rt, rstd, vcol(p - 1, "gm"))
                        post_mixer(p - 1, tile, xt, h)
                        if debug_stop:
                            S.add("sp", lambda e, tile=tile: e.dma_start(
                                out=dbg[:, tile * TT:(tile + 1) * TT].rearrange("(kc p) t -> p kc t", p=128), in_=xt[:]),
                                reads=[("xt", m, sub) for m in range(8) for sub in range(NSUB)],
                                writes=[("dbg", tile)], dma_key="dbg")
                        rmsnorm(xt, h, sq, rt, rstd, vcol(p - 1, "g2"))
                        ffn(p - 1, "b", xt, h)
                    if p < depth:
                        rmsnorm(xt, h, sq, rt, rstd, vcol(p, "g1"))
                        ffn(p, "a", xt, h)
                        rmsnorm(xt, h, sq, rt, rstd, vcol(p, "gm"))
                        S.add("sp", lambda e, tile=tile: e.dma_start(
                            out=xs[:, tile * TT:(tile + 1) * TT].rearrange("(kc p) t -> p kc t", p=128), in_=xt[:]),
                            reads=[("xt", m, sub) for m in range(8) for sub in range(NSUB)],
                            writes=[("xs", tile)], dma_key="xs")
                        pre_mixer(p, tile, h)
                    else:
                        final_norm(tile, xt, h, sq, rt, rstd)
                S.barrier()

        for p in range(depth + 1):
            run_pass(p)
            if p < depth:
                dft(p)
        S.add("sp", None, reads=["yT"])
        S.emit(nc)
    return nc


_CACHE = {}


def kernel(**inputs):
    split = 2
    ncores = BATCH * split
    T = SEQ // split
    inp = {k: np.asarray(v) for k, v in inputs.items()}
    if "nc" not in _CACHE:
        _CACHE["nc"] = build_program(DEPTH, split)
    nc = _CACHE["nc"]
    vecs = pack_vecs(inp, DEPTH)
    shared = {k: np.ascontiguousarray(inp[k], dtype=np.float32) for k in
              ("w1_a", "w3_a", "w2_a", "w1_b", "w3_b", "w2_b", "w_in", "w_fourier", "w_conv_out", "w_pool", "w_o")}
    in_maps = []
    for core in range(ncores):
        bidx, rank = core // split, core % split
        d64, mr, ccsc, edge = host_consts(split, rank)
        xTc = np.ascontiguousarray(inp["x"][bidx, rank * T:(rank + 1) * T, :].T, dtype=np.float32)
        m = dict(shared)
        m.update({"xT": xTc, "vecs": vecs, "d64": d64, "mr": mr, "ccsc": ccsc, "edge": edge})
        if split == 2:
            mk = np.zeros((128, 2), np.float32)
            mk[:, 0] = 1.0 if rank == 1 else 0.0
            mk[:, 1] = 1.0 if rank == 0 else 0.0
            m["masks"] = mk
        in_maps.append(m)
    res = run_bass_kernel_spmd(nc, in_maps, core_ids=list(range(ncores)))
    out = np.empty((BATCH, SEQ, D), np.float32)
    for core in range(ncores):
        bidx, rank = core // split, core % split
        out[bidx, rank * T:(rank + 1) * T, :] = res.results[core]["yT"].T
    return out
```

```python
import contextlib
import numpy as np
import concourse.bass as bass
import concourse.mybir as mybir
from concourse.bass_utils import run_bass_kernel_spmd

F32 = mybir.dt.float32
BF16 = mybir.dt.bfloat16
AF = mybir.ActivationFunctionType
ALU = mybir.AluOpType

D = 1024
DFF = 2816
SEQ = 8192
BATCH = 4
DEPTH = 4
INW = 5632
OFF_F, OFF_B, OFF_C, OFF_V, OFF_P, OFF_G = 0, 512, 1024, 1536, 2048, 2560
EPS = 1e-6
POOLW = (2, 4, 8, 16)

ENGS = ("pe", "act", "dve", "pool", "sp")
COMPUTE = ("pe", "act", "dve", "pool")


class Op:
    __slots__ = ("eng", "fn", "deps", "dma_sem", "dma_val", "signal", "sigval", "is_dma", "inc")

    def __init__(self, eng, fn, is_dma):
        self.eng = eng
        self.fn = fn
        self.deps = []
        self.is_dma = is_dma
        self.dma_sem = None
        self.dma_val = 0
        self.signal = False
        self.sigval = 0


class Sched:
    def __init__(self):
        self.ops = {e: [] for e in ENGS}
        self.last_write = {}
        self.readers = {}
        self.dma_sem_keys = {}
        self.last_dma = {}
        self.all_ops = []
        self.bar_pending = {}

    def add(self, eng, fn, reads=(), writes=(), dma_key=None, inc=16):
        is_dma = dma_key is not None
        op = Op(eng, fn, is_dma)
        deps = list(self.bar_pending.pop(eng, ()))
        for r in reads:
            lw = self.last_write.get(r)
            if lw is not None:
                deps.append(lw)
        for w in writes:
            lw = self.last_write.get(w)
            if lw is not None:
                deps.append(lw)
            deps.extend(self.readers.get(w, {}).values())
        seen = set()
        for d in deps:
            if d is op or id(d) in seen:
                continue
            seen.add(id(d))
            if (not d.is_dma) and d.eng == eng and (not is_dma) and eng == "pe":
                continue
            op.deps.append(d)
        for r in reads:
            self.readers.setdefault(r, {})[(eng if not is_dma else id(op))] = op
        for w in writes:
            self.last_write[w] = op
            self.readers[w] = {}
        if is_dma:
            n = self.dma_sem_keys.get(dma_key, 0) + inc
            self.dma_sem_keys[dma_key] = n
            op.dma_sem = dma_key
            op.dma_val = n
            op.inc = inc
            self.last_dma[dma_key] = op
        self.ops[eng].append(op)
        self.all_ops.append(op)
        return op

    def barrier(self):
        deps = []
        for e in COMPUTE:
            for op in reversed(self.ops[e]):
                if not op.is_dma and op.fn is not None:
                    deps.append(op)
                    break
        deps.extend(self.last_dma.values())
        self.bar_pending = {e: deps for e in ENGS}

    def emit(self, nc):
        for op in self.all_ops:
            for d in op.deps:
                if not d.is_dma:
                    d.signal = True
        for e in ENGS:
            c = 0
            for op in self.ops[e]:
                if op.signal:
                    c += 1
                    op.sigval = c
        EP = 16000
        with contextlib.ExitStack() as st:
            esem = {}
            for e in COMPUTE:
                nsig = sum(1 for op in self.ops[e] if op.signal)
                esem[e] = [st.enter_context(nc.semaphore("s_%s%d" % (e, i)))
                           for i in range(nsig // EP + 1)]
            dsem = {}
            for i, k in enumerate(self.dma_sem_keys):
                dsem[k] = st.enter_context(nc.semaphore("d%d" % i))
            block = st.enter_context(nc.Block())

            def run(engname, eng):
                waited = {}
                for op in self.ops[engname]:
                    for d in op.deps:
                        if d.is_dma:
                            sem, val = dsem[d.dma_sem], d.dma_val
                            key = ("d", d.dma_sem)
                        else:
                            ep = (d.sigval - 1) // EP
                            sem, val = esem[d.eng][ep], d.sigval - ep * EP
                            key = ("e", d.eng, ep)
                        if waited.get(key, 0) >= val:
                            continue
                        waited[key] = val
                        eng.wait_ge(sem, val)
                    if op.fn is None:
                        continue
                    ins = op.fn(eng)
                    if op.is_dma:
                        if op.inc == 1:
                            ins.then_inc(dsem[op.dma_sem])
                        else:
                            ins.then_inc(dsem[op.dma_sem], op.inc)
                    elif op.signal:
                        ins.then_inc(esem[engname][(op.sigval - 1) // EP], 1)

            @block.tensor
            def _(eng):
                run("pe", eng)

            @block.scalar
            def _(eng):
                run("act", eng)

            @block.vector
            def _(eng):
                run("dve", eng)

            @block.gpsimd
            def _(eng):
                run("pool", eng)

            @block.sync
            def _(eng):
                run("sp", eng)


def host_consts(split, rank):
    nm = 128 // split
    a = np.arange(64)[:, None].astype(np.float64)
    r = np.arange(64)[None, :].astype(np.float64)
    th = 2 * np.pi * a * r / 64.0
    d64 = np.concatenate([np.cos(th), -np.sin(th)], axis=1).astype(np.float32)
    b = np.arange(128)[:, None, None].astype(np.float64)
    rr = np.arange(64)[None, :, None].astype(np.float64)
    m = (np.arange(nm) + rank * nm)[None, None, :].astype(np.float64)
    th2 = 2 * np.pi * b * ((rr + 64.0 * m) % SEQ) / SEQ
    c2, s2 = np.cos(th2), np.sin(th2)
    mr = np.zeros((128, 64, 2, 2 * nm), np.float32)
    mr[:, :, 0, :nm] = c2
    mr[:, :, 0, nm:] = -s2
    mr[:, :, 1, :nm] = s2
    mr[:, :, 1, nm:] = c2
    c = np.arange(128)[:, None].astype(np.float64)
    cp = np.arange(128)[None, :].astype(np.float64)
    th3 = 2 * np.pi * c * cp / 128.0
    scale = 1.0 / np.sqrt(SEQ * 128.0)
    ccsc = np.stack([np.cos(th3) * scale, np.sin(th3) * scale], axis=1).astype(np.float32)
    edge = np.zeros((128, 2, 4, 8), np.float32)
    for i, w in enumerate(POOLW):
        half = w // 2
        for t in range(8):
            cl = min(t + half, SEQ) - max(t - half, 0)
            tr = SEQ - 8 + t
            cr = min(tr + half, SEQ) - max(tr - half, 0)
            edge[:, 0, i, t] = (1.0 / cl) if rank == 0 else 1.0 / w
            edge[:, 1, i, t] = (1.0 / cr) if rank == split - 1 else 1.0 / w
    import ml_dtypes
    return d64.astype(ml_dtypes.bfloat16), mr.astype(ml_dtypes.bfloat16), ccsc, edge


def pack_vecs(inp, depth):
    cols = []

    def fm(v):
        return np.ascontiguousarray(v.reshape(8, 128).T)

    for l in range(depth):
        cols.append(fm(inp["g_ffn1"][l]))
        cols.append(fm(inp["g_mix"][l]))
        cols.append(fm(inp["g_ffn2"][l]))
        cols.append(fm(inp["pool_scale"][l]))
        wc = inp["w_conv"][l]
        cols.append(np.ascontiguousarray(wc.reshape(3, 4, 128).transpose(2, 0, 1).reshape(128, 12)))
    cols.append(fm(inp["g_final"]))
    return np.ascontiguousarray(np.concatenate(cols, axis=1).astype(np.float32))


VPL = 8 * 4 + 12


def build_program(depth=DEPTH, split=1, debug_stop=None):
    T = SEQ // split
    NM = 128 // split
    TT = 2048
    NT = T // TT
    NSUB = TT // 512
    nc = bass.Bass("TRN2", target_bir_lowering=False)
    S = Sched()

    def dram_in(name, shape, dt=F32):
        return nc.dram_tensor(name, list(shape), dt, kind="ExternalInput").ap()

    xT = dram_in("xT", [D, T])
    w1 = {"a": dram_in("w1_a", [depth, D, DFF]), "b": dram_in("w1_b", [depth, D, DFF])}
    w3 = {"a": dram_in("w3_a", [depth, D, DFF]), "b": dram_in("w3_b", [depth, D, DFF])}
    w2 = {"a": dram_in("w2_a", [depth, DFF, D]), "b": dram_in("w2_b", [depth, DFF, D])}
    w_in = dram_in("w_in", [depth, D, INW])
    w_fourier = dram_in("w_fourier", [depth, 4, 128, 256])
    w_conv_out = dram_in("w_conv_out", [depth, 512, D])
    w_pool = dram_in("w_pool", [depth, 4, 128, 256])
    w_o = dram_in("w_o", [depth, D, D])
    NV = depth * VPL + 8
    vecs_d = dram_in("vecs", [128, NV])
    d64_d = dram_in("d64", [64, 128], BF16)
    mr_d = dram_in("mr", [128, 64, 2, 2 * NM], BF16)
    ccsc_d = dram_in("ccsc", [128, 2, 128])
    edge_d = dram_in("edge", [128, 2, 4, 8])
    yT = nc.dram_tensor("yT", [D, T], F32, kind="ExternalOutput").ap()

    dk = dict(kind="ExternalOutput") if debug_stop else {}
    xs = nc.dram_tensor("xs", [D, T], F32, **dk).ap()
    pfall_t = [nc.dram_tensor("pfall%d" % g, [SEQ, 128], BF16) for g in range(4)]
    pfall = [t_.ap() for t_ in pfall_t]
    if split == 1:
        pfin_t, pfin = pfall_t, pfall
    else:
        pfin_t = [nc.dram_tensor("pfin%d" % g, [T, 128], BF16) for g in range(4)]
        pfin = [t_.ap() for t_ in pfin_t]
        gin_h_t = nc.dram_tensor("gin_h", [128, 128], BF16)
        gout_h_t = nc.dram_tensor("gout_h", [256, 128], BF16)
        gin_h = gin_h_t.ap().rearrange("p (w ch t) -> p w ch t", w=2, ch=8)
        gout_h = gout_h_t.ap().rearrange("(r p) (w ch t) -> r p w ch t", r=2, w=2, ch=8)
        masks_d = dram_in("masks", [128, 2])
    GROUPS = [[2 * i, 2 * i + 1] for i in range(4)]
    zu = nc.dram_tensor("zu", [D, T + 16], BF16, **dk).ap()
    zu_b = nc.dram_tensor("zu_b", [D, T + 16], BF16).ap()
    yf = nc.dram_tensor("yf", [D, T], F32, **dk).ap()
    dbg = nc.dram_tensor("dbg", [D, T], F32, **dk).ap() if debug_stop else None

    bank_ctr = [0]

    def nextbank():
        bank_ctr[0] = (bank_ctr[0] + 1) % 8
        return bank_ctr[0]

    rot = {}

    def rotate(name, n):
        rot[name] = (rot.get(name, -1) + 1) % n
        return rot[name]

    with contextlib.ExitStack() as top:
        uniq = [0]

        def sbt(st, name, shape, dt):
            uniq[0] += 1
            return st.enter_context(nc.sbuf_tensor("sb%d_%s" % (uniq[0], name), list(shape), dt))

        ps = [top.enter_context(nc.psum_tensor("ps%d" % i, [128, 512], F32)) for i in range(8)]
        vecs = sbt(top, "vecs", [128, NV], F32)
        ones = sbt(top, "ones", [128, 128], BF16)
        edge = sbt(top, "edge", [128, 2, 4, 8], F32)
        zero16 = sbt(top, "zero16", [128, 8, 8], BF16)
        if split == 2:
            masks = sbt(top, "masks", [128, 2], F32)
            S.add("sp", lambda e: e.dma_start(out=masks[:], in_=masks_d), writes=["masks"], dma_key="masks")

        S.add("sp", lambda e: e.dma_start(out=vecs[:], in_=vecs_d), writes=["vecs"], dma_key="vecs")
        S.add("sp", lambda e: e.dma_start(out=edge[:], in_=edge_d), writes=["edge"], dma_key="edge")
        S.add("dve", lambda e: e.memset(ones[:], 1.0 / 1024.0), writes=["ones"])
        S.add("dve", lambda e: e.memset(zero16[:], 0.0), writes=["zero16"])
        zuvs = [zu.rearrange("(ch p) t -> p ch t", p=128), zu_b.rearrange("(ch p) t -> p ch t", p=128)]
        for par in range(2):
            S.add("sp", lambda e, par=par: e.dma_start(out=zuvs[par][:, :, 0:8], in_=zero16[:]), reads=["zero16"],
                  writes=[("zu_hl", par)], dma_key=("zu_hl", par))
            S.add("sp", lambda e, par=par: e.dma_start(out=zuvs[par][:, :, T + 8:T + 16], in_=zero16[:]), reads=["zero16"],
                  writes=[("zu_hr", par)], dma_key=("zu_hr", par))

        def vcol(l, which, k=0):
            base = l * VPL + {"g1": 0, "gm": 8, "g2": 16, "ps": 24, "wc": 32}[which]
            return base + k

        def rmsnorm(xt, h, sq, rt, rstd, gcol):
            for sub in range(NSUB):
                tsl = slice(sub * 512, (sub + 1) * 512)
                q = 0
                S.add("act", lambda e, q=q, tsl=tsl: e.activation(out=sq[q][:], in_=xt[:, :, tsl], func=AF.Square),
                      reads=[("xt", m, sub) for m in range(8)], writes=[("sq", q)])
                bk = nextbank()
                for kc in range(8):
                    S.add("pe", lambda e, q=q, kc=kc, bk=bk: e.matmul(ps[bk][:], ones[:], sq[q][:, kc, :],
                                                                       start=(kc == 0), stop=(kc == 7)),
                          reads=[("sq", q), "ones"], writes=[("ps", bk)])
                S.add("act", lambda e, bk=bk: e.activation(out=rt[:], in_=ps[bk][:], func=AF.Ln, bias=EPS, scale=1.0),
                      reads=[("ps", bk)], writes=["rt"])
                S.add("act", lambda e: e.activation(out=rstd[:], in_=rt[:], func=AF.Exp, scale=-0.5),
                      reads=["rt"], writes=["rstd"])
                for kc in range(8):
                    S.add("dve", lambda e, kc=kc, tsl=tsl: e.scalar_tensor_tensor(
                        out=h[:, kc, tsl], in0=xt[:, kc, tsl], scalar=vecs[:, gcol + kc:gcol + kc + 1], in1=rstd[:],
                        op0=ALU.mult, op1=ALU.mult),
                        reads=[("xt", kc, sub), "rstd", "vecs"], writes=[("h", sub)])

        def ffn(l, ab, xt, h):
            groups = [(0, 4), (4, 4), (8, 4), (12, 4), (16, 4), (20, 2)]
            with contextlib.ExitStack() as st:
                w1b = [sbt(st, "w1b%d" % i, [128, 8, 512], BF16) for i in range(2)]
                w3b = [sbt(st, "w3b%d" % i, [128, 8, 512], BF16) for i in range(2)]
                w2b = [sbt(st, "w2b%d" % i, [128, 4, 1024], BF16) for i in range(2)]
                ub = [sbt(st, "ub%d" % i, [128, 4, TT], BF16) for i in range(2)]
                sg = [sbt(st, "sg%d" % i, [128, 512], F32) for i in range(2)]
                for (j0, nj) in groups:
                    b = rotate("ffnw", 2)
                    c0, c1 = j0 * 128, (j0 + nj) * 128
                    S.add("pool", lambda e, b=b, c0=c0, c1=c1, nj=nj: e.dma_start(
                        out=w1b[b][:, :, 0:nj * 128], in_=w1[ab][l, :, c0:c1].rearrange("(kc p) n -> p kc n", p=128)),
                        writes=[("w1b", b)], dma_key=("w1b", b))
                    S.add("pool", lambda e, b=b, c0=c0, c1=c1, nj=nj: e.dma_start(
                        out=w3b[b][:, :, 0:nj * 128], in_=w3[ab][l, :, c0:c1].rearrange("(kc p) n -> p kc n", p=128)),
                        writes=[("w3b", b)], dma_key=("w3b", b))
                    S.add("pool", lambda e, b=b, c0=c0, c1=c1, nj=nj: e.dma_start(
                        out=w2b[b][:, 0:nj, :], in_=w2[ab][l, c0:c1, :].rearrange("(j p) n -> p j n", p=128)),
                        writes=[("w2b", b)], dma_key=("w2b", b))
                    up_order = ([(jj, sub) for sub in range(NSUB) for jj in range(nj)] if j0 == 0
                                else [(jj, sub) for jj in range(nj) for sub in range(NSUB)])
                    for (jj, sub) in up_order:
                        if True:
                            tsl = slice(sub * 512, (sub + 1) * 512)
                            ba, bb = nextbank(), nextbank()
                            for kc in range(8):
                                S.add("pe", lambda e, b=b, kc=kc, jj=jj, tsl=tsl, ba=ba: e.matmul(
                                    ps[ba][:], w1b[b][:, kc, jj * 128:(jj + 1) * 128], h[:, kc, tsl],
                                    start=(kc == 0), stop=(kc == 7)),
                                    reads=[("w1b", b), ("h", sub)], writes=[("ps", ba)])
                            for kc in range(8):
                                S.add("pe", lambda e, b=b, kc=kc, jj=jj, tsl=tsl, bb=bb: e.matmul(
                                    ps[bb][:], w3b[b][:, kc, jj * 128:(jj + 1) * 128], h[:, kc, tsl],
                                    start=(kc == 0), stop=(kc == 7)),
                                    reads=[("w3b", b), ("h", sub)], writes=[("ps", bb)])
                            k = rotate("sg", 2)
                            S.add("act", lambda e, k=k, ba=ba: e.activation(out=sg[k][:], in_=ps[ba][:], func=AF.Silu),
                                  reads=[("ps", ba)], writes=[("sg", k)])
                            S.add("dve", lambda e, k=k, bb=bb, b=b, jj=jj, tsl=tsl: e.tensor_tensor(
                                out=ub[b][:, jj, tsl], in0=sg[k][:], in1=ps[bb][:], op=ALU.mult),
                                reads=[("sg", k), ("ps", bb)], writes=[("ub", b, jj, sub)])
                    for m in range(8):
                        for sub in range(NSUB):
                            tsl = slice(sub * 512, (sub + 1) * 512)
                            bo = nextbank()
                            for jj in range(nj):
                                S.add("pe", lambda e, b=b, jj=jj, m=m, tsl=tsl, bo=bo, nj=nj: e.matmul(
                                    ps[bo][:], w2b[b][:, jj, m * 128:(m + 1) * 128], ub[b][:, jj, tsl],
                                    start=(jj == 0), stop=(jj == nj - 1)),
                                    reads=[("w2b", b), ("ub", b, jj, sub)], writes=[("ps", bo)])
                            S.add("dve", lambda e, m=m, tsl=tsl, bo=bo: e.scalar_tensor_tensor(
                                out=xt[:, m, tsl], in0=ps[bo][:], scalar=0.5, in1=xt[:, m, tsl],
                                op0=ALU.mult, op1=ALU.add),
                                reads=[("ps", bo), ("xt", m, sub)], writes=[("xt", m, sub)])
            S.barrier()

        def pre_mixer(l, tile, h):
            t0 = tile * TT
            with contextlib.ExitStack() as st:
                winf = sbt(st, "winf", [128, 8, 512], BF16)
                pfst = [sbt(st, "pfst%d" % i, [128, 4, 512], BF16) for i in range(2)]
                wch = {k: [sbt(st, "wch%s%d" % (k, i), [128, 8, 128], BF16) for i in range(2)] for k in "CVP"}
                zst = sbt(st, "zst", [128, 8, TT], BF16)
                ctmp = [sbt(st, "ctmp%d" % i, [128, 512], F32) for i in range(2)]
                S.add("pool", lambda e: e.dma_start(
                    out=winf[:], in_=w_in[l, :, OFF_F:OFF_F + 512].rearrange("(kc p) n -> p kc n", p=128)),
                    writes=["winf"], dma_key="winf")
                for s4 in range(NSUB):
                    k = rotate("pfst", 2)
                    for tb in range(4):
                        tok = s4 * 512 + tb * 128
                        bk = nextbank()
                        for kc in range(8):
                            S.add("pe", lambda e, kc=kc, tok=tok, bk=bk: e.matmul(
                                ps[bk][:], h[:, kc, tok:tok + 128], winf[:, kc, :], start=(kc == 0), stop=(kc == 7)),
                                reads=[("h", s4), "winf"], writes=[("ps", bk)])
                        if tb % 2 == 0:
                            S.add("act", lambda e, k=k, tb=tb, bk=bk: e.copy(out=pfst[k][:, tb, :], in_=ps[bk][:]),
                                  reads=[("ps", bk)], writes=[("pfst", k, tb)])
                        else:
                            S.add("dve", lambda e, k=k, tb=tb, bk=bk: e.tensor_copy(out=pfst[k][:, tb, :], in_=ps[bk][:]),
                                  reads=[("ps", bk)], writes=[("pfst", k, tb)])
                    for tb in range(4):
                        tok = t0 + s4 * 512 + tb * 128
                        for g in range(4):
                            S.add("sp", lambda e, k=k, tb=tb, tok=tok, g=g: e.dma_start(
                                out=pfin[g][tok:tok + 128, :], in_=pfst[k][:, tb, g * 128:(g + 1) * 128]),
                                reads=[("pfst", k, tb)], writes=[("pf", g, tile, s4, tb)], dma_key=("pfst", k, tb))
                def load_cc(cc):
                    b = cc % 2
                    for kind, off in (("C", OFF_C), ("V", OFF_V), ("P", OFF_P)):
                        S.add("pool", lambda e, kind=kind, off=off, b=b, cc=cc: e.dma_start(
                            out=wch[kind][b][:],
                            in_=w_in[l, :, off + cc * 128:off + (cc + 1) * 128].rearrange("(kc p) n -> p kc n", p=128)),
                            writes=[("wch", kind, b)], dma_key=("wch", kind, b))

                load_cc(0)
                load_cc(1)
                if split == 2 and tile == NT - 1:
                    exchange_pf()
                for cc in range(4):
                    b = cc % 2
                    if cc >= 2:
                        load_cc(cc)
                    for sub in range(NSUB):
                        tsl = slice(sub * 512, (sub + 1) * 512)
                        bks = {}
                        for kind in "CVP":
                            bk = nextbank()
                            bks[kind] = bk
                            for kc in range(8):
                                S.add("pe", lambda e, kind=kind, b=b, kc=kc, tsl=tsl, bk=bk: e.matmul(
                                    ps[bk][:], wch[kind][b][:, kc, :], h[:, kc, tsl], start=(kc == 0), stop=(kc == 7)),
                                    reads=[("wch", kind, b), ("h", sub)], writes=[("ps", bk)])
                        k = rotate("ctmp", 2)
                        S.add("act", lambda e, k=k, bk=bks["C"]: e.copy(out=ctmp[k][:], in_=ps[bk][:]),
                              reads=[("ps", bks["C"])], writes=[("ctmp", k)])
                        S.add("dve", lambda e, k=k, bk=bks["V"], cc=cc, tsl=tsl: e.tensor_tensor(
                            out=zst[:, cc, tsl], in0=ctmp[k][:], in1=ps[bk][:], op=ALU.mult),
                            reads=[("ctmp", k), ("ps", bks["V"])], writes=[("zst", cc)])
                        S.add("act", lambda e, bk=bks["P"], cc=cc, tsl=tsl: e.copy(out=zst[:, 4 + cc, tsl], in_=ps[bk][:]),
                              reads=[("ps", bks["P"])], writes=[("zst", 4 + cc)])
                zuv = zuvs[l % 2]
                S.add("sp", lambda e: e.dma_start(out=zuv[:, :, 8 + t0:8 + t0 + TT], in_=zst[:]),
                      reads=[("zst", i) for i in range(8)], writes=[("zu", l % 2, tile)], dma_key="zst")
                if split == 2 and tile == 0:
                    S.add("sp", lambda e: e.dma_start(out=gin_h[:, 0], in_=zst[:, :, 0:8]),
                          reads=[("zst", i) for i in range(8)], writes=["gin_h0"], dma_key="gh0")
                if split == 2 and tile == NT - 1:
                    S.add("sp", lambda e: e.dma_start(out=gin_h[:, 1], in_=zst[:, :, TT - 8:TT]),
                          reads=[("zst", i) for i in range(8)], writes=["gin_h1"], dma_key="gh1")
                    exchange_halo()
            S.barrier()

        def exchange_pf():
            for g in range(4):
                S.add("pool", lambda e, g=g: e.collective_compute(
                    "AllGather", ALU.bypass, replica_groups=GROUPS,
                    ins=[pfin_t[g].ap().opt()], outs=[pfall_t[g].ap().opt()]),
                    reads=[("pf", g, t, a_, b_) for t in range(NT) for a_ in range(4) for b_ in range(4)],
                    writes=[("pfall", g)], dma_key=("cc", g), inc=1)
        def exchange_halo():
            S.add("pool", lambda e: e.collective_compute(
                "AllGather", ALU.bypass, replica_groups=GROUPS,
                ins=[gin_h_t.ap().opt()], outs=[gout_h_t.ap().opt()]),
                reads=["gin_h0", "gin_h1"], writes=["gout_h"], dma_key=("cc", "h"), inc=1)

        def dft(l):
            with contextlib.ExitStack() as st:
                U1 = [sbt(st, "U1_%d" % i, [64, 128, 128], BF16) for i in range(split)]
                G = sbt(st, "G", [128, 128, 128], BF16)
                MRb = sbt(st, "MRb", [128, 64, 2, 2 * NM], BF16)
                Zb = sbt(st, "Zb", [128, 2, T], BF16)
                yst = [sbt(st, "yst%d" % i, [128, 1024], F32) for i in range(2)]
                d64 = sbt(st, "d64", [64, 128], BF16)
                ccsc = sbt(st, "ccsc", [128, 2, 128], F32)
                wf32 = sbt(st, "wf32", [128, 4, 256], F32)
                wcs = sbt(st, "wcs", [128, 4, 2, 256], BF16)
                S.add("sp", lambda e: e.dma_start(out=d64[:], in_=d64_d), writes=["d64"], dma_key="d64")
                S.add("pool", lambda e: e.dma_start(out=MRb[:], in_=mr_d), writes=["MRb"], dma_key="MRb")
                S.add("sp", lambda e: e.dma_start(out=ccsc[:], in_=ccsc_d), writes=["ccsc"], dma_key="ccsc")
                S.add("sp", lambda e: e.dma_start(out=wf32[:], in_=w_fourier[l].rearrange("g c d -> c g d")),
                      writes=["wf32"], dma_key="wf32")
                for g in range(4):
                    for part in range(2):
                        bk = nextbank()
                        S.add("pe", lambda e, g=g, part=part, bk=bk: e.matmul(
                            ps[bk][:, 0:256], ccsc[:, part, :], wf32[:, g, :], start=True, stop=True),
                            reads=["ccsc", "wf32"], writes=[("ps", bk)])
                        S.add("dve", lambda e, g=g, part=part, bk=bk: e.tensor_copy(out=wcs[:, g, part, :], in_=ps[bk][:, 0:256]),
                              reads=[("ps", bk)], writes=[("wcs", g)])
                def load_u1(g):
                    ub_ = g % split
                    S.add("sp", lambda e, g=g, ub_=ub_: e.dma_start(
                        out=U1[ub_][:], in_=pfall[g].rearrange("(a b) c -> a b c", b=128)),
                        reads=([("pf", g, t, a_, b_) for t in range(NT) for a_ in range(4) for b_ in range(4)]
                               if split == 1 else [("pfall", g)]),
                        writes=[("U1", ub_)], dma_key=("U1", ub_))

                if split == 2:
                    load_u1(0)
                for g in range(4):
                    ub_ = g % split
                    if split == 2:
                        if g + 1 < 4:
                            load_u1(g + 1)
                    else:
                        load_u1(g)
                    for c4 in range(32):
                        bk = nextbank()
                        for ci in range(4):
                            c = c4 * 4 + ci
                            S.add("pe", lambda e, ub_=ub_, c=c, ci=ci, bk=bk: e.matmul(
                                ps[bk][:, ci * 128:(ci + 1) * 128], U1[ub_][:, :, c], d64[:], start=True, stop=True),
                                reads=[("U1", ub_), "d64"], writes=[("ps", bk)])
                        outap = G[:, c4 * 4:(c4 + 1) * 4, :]
                        if c4 % 2 == 0:
                            S.add("act", lambda e, outap=outap, bk=bk: e.copy(
                                out=outap, in_=ps[bk][:].rearrange("p (c n) -> p c n", n=128)),
                                reads=[("ps", bk)], writes=[("G", c4)])
                        else:
                            S.add("dve", lambda e, outap=outap, bk=bk: e.tensor_copy(
                                out=outap, in_=ps[bk][:].rearrange("p (c n) -> p c n", n=128)),
                                reads=[("ps", bk)], writes=[("G", c4)])
                    Zv = Zb[:].rearrange("p t (m r) -> p t m r", r=64)
                    allG = [("G", i) for i in range(32)]
                    for r in range(64):
                        bk = nextbank()
                        for part in range(2):
                            S.add("pe", lambda e, r=r, part=part, bk=bk: e.matmul(
                                ps[bk][:, 0:2 * NM], G[:, :, part * 64 + r], MRb[:, r, part, :],
                                start=(part == 0), stop=(part == 1)),
                                reads=allG + ["MRb"], writes=[("ps", bk)])
                        src = lambda bk=bk: ps[bk][:, 0:2 * NM].rearrange("p (t m) -> p t m", t=2)
                        if r % 2 == 0:
                            S.add("act", lambda e, r=r, src=src: e.copy(out=Zv[:, :, :, r], in_=src()),
                                  reads=[("ps", bk)], writes=[("Z", r)])
                        else:
                            S.add("dve", lambda e, r=r, src=src: e.tensor_copy(out=Zv[:, :, :, r], in_=src()),
                                  reads=[("ps", bk)], writes=[("Z", r)])
                    allZ = [("Z", i) for i in range(64)]
                    for dc in range(2):
                        for k4 in range(T // 1024):
                            ys = rotate("yst", 2)
                            for kk in range(2):
                                kt = k4 * 2 + kk
                                ksl = slice(kt * 512, (kt + 1) * 512)
                                bk = nextbank()
                                for part in range(2):
                                    S.add("pe", lambda e, g=g, dc=dc, part=part, ksl=ksl, bk=bk: e.matmul(
                                        ps[bk][:], wcs[:, g, part, dc * 128:(dc + 1) * 128], Zb[:, part, ksl],
                                        start=(part == 0), stop=(part == 1)),
                                        reads=allZ + [("wcs", g)], writes=[("ps", bk)])
                                if kk % 2 == 0:
                                    S.add("act", lambda e, ys=ys, kk=kk, bk=bk: e.copy(
                                        out=yst[ys][:, kk * 512:(kk + 1) * 512], in_=ps[bk][:]),
                                        reads=[("ps", bk)], writes=[("yst", ys, kk)])
                                else:
                                    S.add("dve", lambda e, ys=ys, kk=kk, bk=bk: e.tensor_copy(
                                        out=yst[ys][:, kk * 512:(kk + 1) * 512], in_=ps[bk][:]),
                                        reads=[("ps", bk)], writes=[("yst", ys, kk)])
                            row = (2 * g + dc) * 128
                            S.add("sp", lambda e, ys=ys, row=row, k4=k4: e.dma_start(
                                out=yf[row:row + 128, k4 * 1024:(k4 + 1) * 1024], in_=yst[ys][:]),
                                reads=[("yst", ys, i) for i in range(2)], writes=[("yf", 2 * g + dc, k4)],
                                dma_key=("yst", ys))
            S.barrier()

        def post_mixer(l, tile, xt, h):
            for hf in range(TT // 1024):
                post_half(l, tile, xt, h, hf)

        def post_half(l, tile, xt, h, hf):
            HT = 1024
            if True:
                t0 = tile * TT + hf * HT
                with contextlib.ExitStack() as st:
                    zub = sbt(st, "zub", [128, 8, HT + 16], BF16)
                    bcb = sbt(st, "bcb", [128, 4, HT], BF16)
                    plb = sbt(st, "plb", [128, 4, HT], BF16)
                    mgb = sbt(st, "mgb", [128, 8, HT], BF16)
                    yfb = [sbt(st, "yfb%d" % i, [128, 512], F32) for i in range(2)]
                    wB = [sbt(st, "wB%d" % i, [128, 8, 128], BF16) for i in range(2)]
                    wg = [[sbt(st, "wg%d_%d" % (br, i), [128, 8, 128], BF16) for i in range(2)] for br in range(3)]
                    wco = [sbt(st, "wco%d" % i, [128, 4, 128], BF16) for i in range(2)]
                    wpl = sbt(st, "wpl", [128, 4, 256], BF16)
                    woc = [sbt(st, "woc%d" % i, [128, 8, 128], BF16) for i in range(2)]
                    cv = [sbt(st, "cv%d" % i, [128, 512], F32) for i in range(2)]
                    pa = sbt(st, "pa", [128, 528], F32)
                    pb = sbt(st, "pb", [128, 528], F32)
                    e8 = sbt(st, "e8", [128, 8], F32)
                    gsb = [[sbt(st, "gsb%d_%d" % (br, i), [128, 512], F32) for i in range(1)] for br in range(3)]
                    t1 = [sbt(st, "t1_%d" % i, [128, 512], F32) for i in range(1)]
                    t2 = [sbt(st, "t2_%d" % i, [128, 512], F32) for i in range(1)]

                    zuv = zuvs[l % 2]
                    S.add("sp", lambda e: e.dma_start(out=zub[:], in_=zuv[:, :, t0:t0 + HT + 16]),
                          reads=[("zu", l % 2, t) for t in range(NT)] + [("zu_hl", l % 2), ("zu_hr", l % 2)],
                          writes=["zub"], dma_key="zub")
                    if split == 2 and (t0 == 0 or t0 + HT == T):
                        hst = sbt(st, "hst", [128, 8, 8], BF16)
                        if t0 == 0:
                            S.add("sp", lambda e: e.dma_start(out=hst[:], in_=gout_h[0, :, 1]),
                                  reads=["gout_h"], writes=["hst"], dma_key="hst")
                            S.add("dve", lambda e: e.tensor_scalar(
                                out=zub[:, :, 0:8], in0=hst[:], scalar1=masks[:, 0:1], scalar2=None, op0=ALU.mult),
                                reads=["hst", "masks", "zub"], writes=["zub"])
                        else:
                            S.add("sp", lambda e: e.dma_start(out=hst[:], in_=gout_h[1, :, 0]),
                                  reads=["gout_h"], writes=["hst"], dma_key="hst")
                            S.add("dve", lambda e: e.tensor_scalar(
                                out=zub[:, :, HT + 8:HT + 16], in0=hst[:], scalar1=masks[:, 1:2], scalar2=None, op0=ALU.mult),
                                reads=["hst", "masks", "zub"], writes=["zub"])
                    S.add("pool", lambda e: e.dma_start(out=wpl[:], in_=w_pool[l].rearrange("g c d -> c g d")),
                          writes=["wpl"], dma_key="wpl")
                    for cc in range(4):
                        b = rotate("wB", 2)
                        S.add("pool", lambda e, b=b, cc=cc: e.dma_start(
                            out=wB[b][:],
                            in_=w_in[l, :, OFF_B + cc * 128:OFF_B + (cc + 1) * 128].rearrange("(kc p) n -> p kc n", p=128)),
                            writes=[("wB", b)], dma_key=("wB", b))
                        for s2 in range(2):
                            sub = hf * 2 + s2
                            o = 8 + s2 * 512
                            k = rotate("cv", 2)
                            wc0 = vcol(l, "wc", 0 * 4 + cc)
                            wc1 = vcol(l, "wc", 1 * 4 + cc)
                            wc2 = vcol(l, "wc", 2 * 4 + cc)
                            S.add("dve", lambda e, k=k, cc=cc, o=o, wc1=wc1: e.tensor_scalar(
                                out=cv[k][:], in0=zub[:, cc, o:o + 512], scalar1=vecs[:, wc1:wc1 + 1], scalar2=None,
                                op0=ALU.mult), reads=["zub", "vecs"], writes=[("cv", k)])
                            S.add("dve", lambda e, k=k, cc=cc, o=o, wc0=wc0: e.scalar_tensor_tensor(
                                out=cv[k][:], in0=zub[:, cc, o - 1:o + 511], scalar=vecs[:, wc0:wc0 + 1], in1=cv[k][:],
                                op0=ALU.mult, op1=ALU.add), reads=["zub", "vecs", ("cv", k)], writes=[("cv", k)])
                            S.add("dve", lambda e, k=k, cc=cc, o=o, wc2=wc2: e.scalar_tensor_tensor(
                                out=cv[k][:], in0=zub[:, cc, o + 1:o + 513], scalar=vecs[:, wc2:wc2 + 1], in1=cv[k][:],
                                op0=ALU.mult, op1=ALU.add), reads=["zub", "vecs", ("cv", k)], writes=[("cv", k)])
                            bk = nextbank()
                            tsl = slice(sub * 512, (sub + 1) * 512)
                            for kc in range(8):
                                S.add("pe", lambda e, b=b, kc=kc, tsl=tsl, bk=bk: e.matmul(
                                    ps[bk][:], wB[b][:, kc, :], h[:, kc, tsl], start=(kc == 0), stop=(kc == 7)),
                                    reads=[("wB", b), ("h", sub)], writes=[("ps", bk)])
                            S.add("dve", lambda e, k=k, cc=cc, s2=s2, bk=bk: e.tensor_tensor(
                                out=bcb[:, cc, s2 * 512:(s2 + 1) * 512], in0=cv[k][:], in1=ps[bk][:], op=ALU.mult),
                                reads=[("cv", k), ("ps", bk)], writes=[("bcb", s2)])
                    for i, w in enumerate(POOLW):
                        for s2 in range(2):
                            o = s2 * 512
                            ue = lambda lo, hi, i=i, o=o: zub[:, 4 + i, o + lo:o + hi]
                            R = ["zub"]
                            if i == 0:
                                S.add("dve", lambda e, ue=ue: e.tensor_tensor(out=pa[:, 8:520], in0=ue(7, 519), in1=ue(8, 520), op=ALU.add),
                                      reads=R, writes=["pa"])
                                win = pa
                            else:
                                S.add("dve", lambda e, ue=ue: e.tensor_tensor(out=pa[:, 1:528], in0=ue(0, 527), in1=ue(1, 528), op=ALU.add),
                                      reads=R, writes=["pa"])
                                S.add("dve", lambda e: e.tensor_tensor(out=pb[:, 2:527], in0=pa[:, 1:526], in1=pa[:, 3:528], op=ALU.add),
                                      reads=["pa"], writes=["pb"])
                                win = pb
                                if i >= 2:
                                    S.add("dve", lambda e: e.tensor_tensor(out=pa[:, 4:525], in0=pb[:, 2:523], in1=pb[:, 6:527], op=ALU.add),
                                          reads=["pb"], writes=["pa"])
                                    win = pa
                                if i >= 3:
                                    S.add("dve", lambda e: e.tensor_tensor(out=pb[:, 8:521], in0=pa[:, 4:517], in1=pa[:, 12:525], op=ALU.add),
                                          reads=["pa"], writes=["pb"])
                                    win = pb
                            wname = "pa" if win is pa else "pb"
                            S.add("dve", lambda e, win=win, ue=ue, i=i, s2=s2, w=w: e.scalar_tensor_tensor(
                                out=plb[:, i, s2 * 512:(s2 + 1) * 512], in0=win[:, 8:520], scalar=1.0 / w, in1=ue(8, 520),
                                op0=ALU.mult, op1=ALU.subtract), reads=[wname, "zub"], writes=[("plb", s2)])
                            if t0 == 0 and s2 == 0:
                                S.add("dve", lambda e, win=win, i=i: e.tensor_tensor(
                                    out=e8[:], in0=win[:, 8:16], in1=edge[:, 0, i, :], op=ALU.mult),
                                    reads=[wname, "edge"], writes=["e8"])
                                S.add("dve", lambda e, ue=ue, i=i: e.tensor_tensor(
                                    out=plb[:, i, 0:8], in0=e8[:], in1=ue(8, 16), op=ALU.subtract),
                                    reads=["e8", "zub"], writes=[("plb", s2)])
                            if t0 + HT == T and s2 == 1:
                                S.add("dve", lambda e, win=win, i=i: e.tensor_tensor(
                                    out=e8[:], in0=win[:, 512:520], in1=edge[:, 1, i, :], op=ALU.mult),
                                    reads=[wname, "edge"], writes=["e8"])
                                S.add("dve", lambda e, ue=ue, i=i: e.tensor_tensor(
                                    out=plb[:, i, HT - 8:HT], in0=e8[:], in1=ue(512, 520), op=ALU.subtract),
                                    reads=["e8", "zub"], writes=[("plb", s2)])
                    for m in range(8):
                        b = rotate("wg", 2)
                        for br in range(3):
                            col = OFF_G + br * 1024 + m * 128
                            S.add("pool", lambda e, br=br, b=b, col=col: e.dma_start(
                                out=wg[br][b][:], in_=w_in[l, :, col:col + 128].rearrange("(kc p) n -> p kc n", p=128)),
                                writes=[("wg", br, b)], dma_key=("wg", br, b))
                        S.add("pool", lambda e, b=b, m=m: e.dma_start(
                            out=wco[b][:], in_=w_conv_out[l, :, m * 128:(m + 1) * 128].rearrange("(j p) n -> p j n", p=128)),
                            writes=[("wco", b)], dma_key=("wco", b))
                        for s2 in range(2):
                            yb = rotate("yfb", 2)
                            S.add("sp", lambda e, yb=yb, m=m, s2=s2: e.dma_start(
                                out=yfb[yb][:], in_=yf[m * 128:(m + 1) * 128, t0 + s2 * 512:t0 + (s2 + 1) * 512]),
                                reads=[("yf", m, t0 // 1024)], writes=[("yfb", yb)], dma_key=("yfb", yb))
                            sub = hf * 2 + s2
                            tsl = slice(sub * 512, (sub + 1) * 512)
                            hsl = slice(s2 * 512, (s2 + 1) * 512)
                            gb = []
                            for br in range(3):
                                bk = nextbank()
                                gb.append(bk)
                                for kc in range(8):
                                    S.add("pe", lambda e, br=br, b=b, kc=kc, tsl=tsl, bk=bk: e.matmul(
                                        ps[bk][:], wg[br][b][:, kc, :], h[:, kc, tsl], start=(kc == 0), stop=(kc == 7)),
                                        reads=[("wg", br, b), ("h", sub)], writes=[("ps", bk)])
                            byc = nextbank()
                            for j in range(4):
                                S.add("pe", lambda e, j=j, b=b, hsl=hsl, byc=byc: e.matmul(
                                    ps[byc][:], wco[b][:, j, :], bcb[:, j, hsl], start=(j == 0), stop=(j == 3)),
                                    reads=[("wco", b), ("bcb", s2)], writes=[("ps", byc)])
                            byp = nextbank()
                            S.add("pe", lambda e, m=m, hsl=hsl, byp=byp: e.matmul(
                                ps[byp][:], wpl[:, m // 2, (m % 2) * 128:(m % 2 + 1) * 128], plb[:, m // 2, hsl],
                                start=True, stop=True),
                                reads=["wpl", ("plb", s2)], writes=[("ps", byp)])
                            k = 0
                            kk2 = 0
                            for br in range(3):
                                S.add("act", lambda e, br=br, k=k, bk=gb[br]: e.activation(
                                    out=gsb[br][k][:], in_=ps[bk][:], func=AF.Sigmoid),
                                    reads=[("ps", gb[br])], writes=[("gsb", br, k)])
                            psc = vcol(l, "ps", m)
                            q2 = kk2
                            S.add("dve", lambda e, q2=q2, byc=byc: e.tensor_tensor(
                                out=t1[q2][:], in0=gsb[1][0][:], in1=ps[byc][:], op=ALU.mult),
                                reads=[("gsb", 1, 0), ("ps", byc)], writes=[("t1", q2)])
                            S.add("dve", lambda e, q2=q2, byp=byp, psc=psc: e.scalar_tensor_tensor(
                                out=t2[q2][:], in0=ps[byp][:], scalar=vecs[:, psc:psc + 1], in1=gsb[2][0][:],
                                op0=ALU.mult, op1=ALU.mult),
                                reads=[("gsb", 2, 0), ("ps", byp), "vecs"], writes=[("t2", q2)])
                            S.add("dve", lambda e, q2=q2: e.tensor_tensor(out=t1[q2][:], in0=t1[q2][:], in1=t2[q2][:], op=ALU.add),
                                  reads=[("t1", q2), ("t2", q2)], writes=[("t1", q2)])
                            S.add("dve", lambda e, q2=q2, yb=yb: e.tensor_tensor(
                                out=t2[q2][:], in0=gsb[0][0][:], in1=yfb[yb][:], op=ALU.mult),
                                reads=[("gsb", 0, 0), ("yfb", yb)], writes=[("t2", q2)])
                            S.add("dve", lambda e, q2=q2, m=m, hsl=hsl: e.tensor_tensor(
                                out=mgb[:, m, hsl], in0=t1[q2][:], in1=t2[q2][:], op=ALU.add),
                                reads=[("t1", q2), ("t2", q2)], writes=[("mgb", m, s2)])
                    for m2 in range(8):
                        b = rotate("woc", 2)
                        S.add("pool", lambda e, b=b, m2=m2: e.dma_start(
                            out=woc[b][:], in_=w_o[l, :, m2 * 128:(m2 + 1) * 128].rearrange("(kc p) n -> p kc n", p=128)),
                            writes=[("woc", b)], dma_key=("woc", b))
                        for s2 in range(2):
                            sub = hf * 2 + s2
                            tsl = slice(sub * 512, (sub + 1) * 512)
                            hsl = slice(s2 * 512, (s2 + 1) * 512)
                            bk = nextbank()
                            for kc in range(8):
                                S.add("pe", lambda e, b=b, kc=kc, hsl=hsl, bk=bk: e.matmul(
                                    ps[bk][:], woc[b][:, kc, :], mgb[:, kc, hsl], start=(kc == 0), stop=(kc == 7)),
                                    reads=[("woc", b), ("mgb", kc, s2)], writes=[("ps", bk)])
                            S.add("dve", lambda e, m2=m2, tsl=tsl, bk=bk: e.tensor_tensor(
                                out=xt[:, m2, tsl], in0=ps[bk][:], in1=xt[:, m2, tsl], op=ALU.add),
                                reads=[("ps", bk), ("xt", m2, sub)], writes=[("xt", m2, sub)])
                S.barrier()

        def final_norm(tile, xt, h_unused, sq, rt, rstd):
            gcol = depth * VPL
            with contextlib.ExitStack() as st:
                ost = [sbt(st, "ost%d" % i, [128, 8, 512], F32) for i in range(2)]
                for sub in range(NSUB):
                    tsl = slice(sub * 512, (sub + 1) * 512)
                    q = 0
                    S.add("act", lambda e, q=q, tsl=tsl: e.activation(out=sq[q][:], in_=xt[:, :, tsl], func=AF.Square),
                          reads=[("xt", m, sub) for m in range(8)], writes=[("sq", q)])
                    bk = nextbank()
                    for kc in range(8):
                        S.add("pe", lambda e, q=q, kc=kc, bk=bk: e.matmul(ps[bk][:], ones[:], sq[q][:, kc, :],
                                                                           start=(kc == 0), stop=(kc == 7)),
                              reads=[("sq", q), "ones"], writes=[("ps", bk)])
                    S.add("act", lambda e, bk=bk: e.activation(out=rt[:], in_=ps[bk][:], func=AF.Ln, bias=EPS, scale=1.0),
                          reads=[("ps", bk)], writes=["rt"])
                    S.add("act", lambda e: e.activation(out=rstd[:], in_=rt[:], func=AF.Exp, scale=-0.5),
                          reads=["rt"], writes=["rstd"])
                    o = rotate("ost", 2)
                    for kc in range(8):
                        S.add("dve", lambda e, kc=kc, tsl=tsl, o=o: e.scalar_tensor_tensor(
                            out=ost[o][:, kc, :], in0=xt[:, kc, tsl], scalar=vecs[:, gcol + kc:gcol + kc + 1], in1=rstd[:],
                            op0=ALU.mult, op1=ALU.mult),
                            reads=[("xt", kc, sub), "rstd", "vecs"], writes=[("ost", o)])
                    c0 = tile * TT + sub * 512
                    S.add("sp", lambda e, o=o, c0=c0: e.dma_start(
                        out=yT[:, c0:c0 + 512].rearrange("(kc p) t -> p kc t", p=128), in_=ost[o][:]),
                        reads=[("ost", o)], writes=["yT"], dma_key=("ost", o))
            S.barrier()

        def run_pass(p):
            with contextlib.ExitStack() as pst:
                xt = sbt(pst, "xt", [128, 8, TT], F32)
                h = sbt(pst, "h", [128, 8, TT], BF16)
                sq = [sbt(pst, "sq%d" % i, [128, 8, 512], BF16) for i in range(1)]
                rt = sbt(pst, "rt", [128, 512], F32)
                rstd = sbt(pst, "rstd", [128, 512], F32)
                for tile in range(NT):
                    src = xT if p == 0 else xs
                    for sub in range(NSUB):
                        c0 = tile * TT + sub * 512
                        S.add("sp", lambda e, src=src, c0=c0, sub=sub: e.dma_start(
                            out=xt[:, :, sub * 512:(sub + 1) * 512],
                            in_=src[:, c0:c0 + 512].rearrange("(kc p) t -> p kc t", p=128)),
                            reads=[("xs", tile)],
                            writes=[("xt", m, sub) for m in range(8)], dma_key=("xt", sub))
                    if p > 0:
                        rmsnorm(xt, h, sq, rt, rstd, vcol(p - 1, "gm"))
                        post_mixer(p - 1, tile, xt, h)
                        if debug_stop:
                            S.add("sp", lambda e, tile=tile: e.dma_start(
                                out=dbg[:, tile * TT:(tile + 1) * TT].rearrange("(kc p) t -> p kc t", p=128), in_=xt[:]),
                                reads=[("xt", m, sub) for m in range(8) for sub in range(NSUB)],
                                writes=[("dbg", tile)], dma_key="dbg")
                        rmsnorm(xt, h, sq, rt, rstd, vcol(p - 1, "g2"))
                        ffn(p - 1, "b", xt, h)
                    if p < depth:
                        rmsnorm(xt, h, sq, rt, rstd, vcol(p, "g1"))
                        ffn(p, "a", xt, h)
                        rmsnorm(xt, h, sq, rt, rstd, vcol(p, "gm"))
                        S.add("sp", lambda e, tile=tile: e.dma_start(
                            out=xs[:, tile * TT:(tile + 1) * TT].rearrange("(kc p) t -> p kc t", p=128), in_=xt[:]),
                            reads=[("xt", m, sub) for m in range(8) for sub in range(NSUB)],
                            writes=[("xs", tile)], dma_key="xs")
                        pre_mixer(p, tile, h)
                    else:
                        final_norm(tile, xt, h, sq, rt, rstd)
                S.barrier()

        for p in range(depth + 1):
            run_pass(p)
            if p < depth:
                dft(p)
        S.add("sp", None, reads=["yT"])
        S.emit(nc)
    return nc


_CACHE = {}


def kernel(**inputs):
    split = 2
    ncores = BATCH * split
    T = SEQ // split
    inp = {k: np.asarray(v) for k, v in inputs.items()}
    if "nc" not in _CACHE:
        _CACHE["nc"] = build_program(DEPTH, split)
    nc = _CACHE["nc"]
    vecs = pack_vecs(inp, DEPTH)
    shared = {k: np.ascontiguousarray(inp[k], dtype=np.float32) for k in
              ("w1_a", "w3_a", "w2_a", "w1_b", "w3_b", "w2_b", "w_in", "w_fourier", "w_conv_out", "w_pool", "w_o")}
    in_maps = []
    for core in range(ncores):
        bidx, rank = core // split, core % split
        d64, mr, ccsc, edge = host_consts(split, rank)
        xTc = np.ascontiguousarray(inp["x"][bidx, rank * T:(rank + 1) * T, :].T, dtype=np.float32)
        m = dict(shared)
        m.update({"xT": xTc, "vecs": vecs, "d64": d64, "mr": mr, "ccsc": ccsc, "edge": edge})
        if split == 2:
            mk = np.zeros((128, 2), np.float32)
            mk[:, 0] = 1.0 if rank == 1 else 0.0
            mk[:, 1] = 1.0 if rank == 0 else 0.0
            m["masks"] = mk
        in_maps.append(m)
    res = run_bass_kernel_spmd(nc, in_maps, core_ids=list(range(ncores)))
    out = np.empty((BATCH, SEQ, D), np.float32)
    for core in range(ncores):
        bidx, rank = core // split, core % split
        out[bidx, rank * T:(rank + 1) * T, :] = res.results[core]["yT"].T
    return out
```
